# Optimizing a Trainium2 kernel written in Bass

```python
import math
import jax
import jax.numpy as jnp
from jax import lax
import numpy as np

D_MODEL = 2048
BATCH = 4
SEQ = 2048
DEPTH = 2

RMS_EPS = 1e-6
PLE_DIM = 256
SSD_HEAD_DIM = 64
SSD_INNER = D_MODEL
SSD_HEADS = SSD_INNER // SSD_HEAD_DIM
SSD_GROUPS = 4
SSD_STATE = 128
SSD_BC = SSD_GROUPS * SSD_STATE
SSD_CHUNK = 256
CONV_K = 4
CONV_CH = SSD_INNER + 2 * SSD_BC
POOL_WINDOWS = (2, 4, 8, 16)
POOL_WIDTH = D_MODEL
POOL_GROUPS = len(POOL_WINDOWS)
POOL_GROUP_W = POOL_WIDTH // POOL_GROUPS
ATTN_PATTERNS = ((128, 1), (512, 4), (2048, 16))
ATTN_HEAD_DIM = 128
ATTN_HEADS_PER_GROUP = 4
ATTN_HEADS = ATTN_HEADS_PER_GROUP * len(ATTN_PATTERNS)
ATTN_QKV_W = ATTN_HEADS * ATTN_HEAD_DIM
ATTN_OUT = ATTN_HEADS_PER_GROUP * ATTN_HEAD_DIM
ATTN_BLOCK = 128
N_BRANCHES = 3
IN_SIZES = (SSD_INNER, CONV_CH, SSD_HEADS,
            POOL_WIDTH, POOL_WIDTH,
            ATTN_QKV_W, ATTN_QKV_W, ATTN_QKV_W,
            ATTN_OUT,
            N_BRANCHES * D_MODEL)
N_IN = sum(IN_SIZES)
IN_SPLITS = tuple(int(s) for s in np.cumsum(IN_SIZES)[:-1])

kernel_name = "hybrid_ssd_pool_dilated_attn_gated_merge"


def _rms_norm(x, g):
    x32 = x.astype(jnp.float32)
    y = x32 * lax.rsqrt(jnp.mean(x32 * x32, axis=-1, keepdims=True) + RMS_EPS)
    return y.astype(x.dtype) * g


def _causal_depthwise_conv(u, w, bias):
    k, c = w.shape
    out = lax.conv_general_dilated(u, w[:, None, :], window_strides=(1,), padding=[(k - 1, 0)],
                                   dimension_numbers=("NWC", "WIO", "NWC"), feature_group_count=c)
    return out + bias


def _segsum(a):
    t = a.shape[-1]
    cs = jnp.cumsum(a, axis=-1)
    seg = cs[..., :, None] - cs[..., None, :]
    mask = jnp.tril(jnp.ones((t, t), dtype=bool))
    return jnp.where(mask, seg, -jnp.inf)


def _ssd_scan(xh, da, bm, cm):
    b, seq, nh, hp = xh.shape
    ng, ns = bm.shape[2], bm.shape[3]
    r = nh // ng
    q = math.gcd(SSD_CHUNK, seq)
    nc = seq // q
    x_c = xh.reshape(b, nc, q, ng, r, hp)
    a_c = da.reshape(b, nc, q, ng, r).transpose(0, 3, 4, 1, 2)
    b_c = bm.reshape(b, nc, q, ng, ns)
    c_c = cm.reshape(b, nc, q, ng, ns)
    a_cum = jnp.cumsum(a_c, axis=-1)
    l_mat = jnp.exp(_segsum(a_c))
    cb = jnp.einsum("bclgn,bcsgn->bgcls", c_c, b_c)
    y_diag = jnp.einsum("bgrcls,bcsgrp->bclgrp", cb[:, :, None] * l_mat, x_c)
    decay_states = jnp.exp(a_cum[..., -1:] - a_cum)
    states = jnp.einsum("bclgn,bgrcl,bclgrp->bcgrpn", b_c, decay_states, x_c)
    states = jnp.concatenate([jnp.zeros_like(states[:, :1]), states], axis=1)
    chunk_decay = jnp.exp(_segsum(jnp.pad(a_cum[..., -1], ((0, 0), (0, 0), (0, 0), (1, 0)))))
    states = jnp.einsum("bgrzc,bcgrpn->bzgrpn", chunk_decay, states)[:, :-1]
    y_off = jnp.einsum("bclgn,bcgrpn,bgrcl->bclgrp", c_c, states, jnp.exp(a_cum))
    return (y_diag + y_off).reshape(b, seq, nh, hp)


def _ssd_branch(z, xbc, dt, conv_w, conv_b, dt_bias, a_log, d_skip, norm_g):
    b, seq, _ = z.shape
    xbc = jax.nn.silu(_causal_depthwise_conv(xbc, conv_w, conv_b))
    xs = xbc[..., :SSD_INNER].reshape(b, seq, SSD_HEADS, SSD_HEAD_DIM)
    bm = xbc[..., SSD_INNER:SSD_INNER + SSD_BC].reshape(b, seq, SSD_GROUPS, SSD_STATE)
    cm = xbc[..., SSD_INNER + SSD_BC:].reshape(b, seq, SSD_GROUPS, SSD_STATE)
    dt32 = jax.nn.softplus(dt.astype(jnp.float32) + dt_bias.astype(jnp.float32))
    a = -jnp.exp(a_log.astype(jnp.float32))
    xs32 = xs.astype(jnp.float32)
    y = _ssd_scan(xs32 * dt32[..., None], dt32 * a, bm.astype(jnp.float32), cm.astype(jnp.float32))
    y = y + d_skip.astype(jnp.float32)[:, None] * xs32
    y = y.reshape(b, seq, SSD_INNER).astype(z.dtype)
    return _rms_norm(y * jax.nn.silu(z), norm_g)


def _pool_branch(u, z, pool_w, pool_scale):
    b, seq, _ = u.shape
    ug = u.reshape(b, seq, POOL_GROUPS, POOL_GROUP_W).astype(jnp.float32)
    cs = jnp.cumsum(ug, axis=1)
    pos = jnp.arange(seq)
    pooled = []
    for gi, w in enumerate(POOL_WINDOWS):
        c = cs[:, :, gi]
        shifted = jnp.pad(c, ((0, 0), (w, 0), (0, 0)))[:, :seq]
        count = jnp.minimum(pos + 1, w).astype(jnp.float32)[:, None]
        pooled.append((c - shifted) / count)
    pooled = (jnp.stack(pooled, axis=2) - ug).astype(u.dtype)
    mixed = jnp.einsum("blgc,gcd->blgd", pooled, pool_w).reshape(b, seq, POOL_WIDTH)
    return mixed * pool_scale * jax.nn.silu(z)


def _dilated_window_attn(q, k, v, window, dilation):
    b, seq, nh, hd = q.shape
    wk = window // dilation
    ld = seq // dilation
    bq = math.gcd(ATTN_BLOCK, ld)
    nb = ld // bq

    def strided(t):
        return t.reshape(b, ld, dilation, nh, hd).transpose(0, 2, 3, 1, 4)

    qs, ks, vs = strided(q), strided(k), strided(v)
    qb = qs.reshape(b, dilation, nh, nb, bq, hd)
    pad = ((0, 0), (0, 0), (0, 0), (wk, 0), (0, 0))
    idx = jnp.arange(nb)[:, None] * bq + jnp.arange(bq + wk)[None, :]
    kb = jnp.pad(ks, pad)[:, :, :, idx]
    vb = jnp.pad(vs, pad)[:, :, :, idx]
    s = jnp.einsum("bdhnqc,bdhnkc->bdhnqk", qb, kb).astype(jnp.float32) * (hd ** -0.5)
    qi = jnp.arange(bq)[:, None]
    kj = jnp.arange(bq + wk)[None, :]
    kpos = jnp.arange(nb)[:, None, None] * bq + kj[None] - wk
    valid = (kj >= qi)[None] & (kj <= qi + wk)[None] & (kpos >= 0)
    s = jnp.where(valid, s, -jnp.inf)
    m = jnp.max(s, axis=-1, keepdims=True)
    e = jnp.exp(s - m)
    den = jnp.sum(e, axis=-1, keepdims=True)
    o = jnp.einsum("bdhnqk,bdhnkc->bdhnqc", (e / den).astype(v.dtype), vb)
    lse = (m + jnp.log(den))[..., 0]
    o = o.reshape(b, dilation, nh, ld, hd).transpose(0, 3, 1, 2, 4).reshape(b, seq, nh, hd)
    lse = lse.reshape(b, dilation, nh, ld).transpose(0, 3, 1, 2).reshape(b, seq, nh)
    return o, lse


def _attn_branch(q, k, v, z, q_norm_g, k_norm_g):
    b, seq, _ = q.shape
    q = _rms_norm(q.reshape(b, seq, ATTN_HEADS, ATTN_HEAD_DIM), q_norm_g)
    k = _rms_norm(k.reshape(b, seq, ATTN_HEADS, ATTN_HEAD_DIM), k_norm_g)
    v = v.reshape(b, seq, ATTN_HEADS, ATTN_HEAD_DIM)
    outs, lses = [], []
    for gi, (window, dilation) in enumerate(ATTN_PATTERNS):
        hs = slice(gi * ATTN_HEADS_PER_GROUP, (gi + 1) * ATTN_HEADS_PER_GROUP)
        o, lse = _dilated_window_attn(q[:, :, hs], k[:, :, hs], v[:, :, hs], window, dilation)
        outs.append(o)
        lses.append(lse)
    wts = jax.nn.softmax(jnp.stack(lses, axis=0), axis=0)
    comb = jnp.sum(wts[..., None].astype(v.dtype) * jnp.stack(outs, axis=0), axis=0)
    return comb.reshape(b, seq, ATTN_OUT) * jax.nn.silu(z)


def setup_inputs(seed: int = 0) -> dict:
    key = jax.random.key(seed)
    ks = jax.random.split(key, 24)
    f32 = jnp.float32

    def nrm(k, shape, scale):
        return jax.random.normal(k, shape, f32) * scale

    dt0 = jnp.exp(jax.random.uniform(ks[6], (DEPTH, SSD_HEADS), f32,
                                     minval=math.log(1e-3), maxval=math.log(1e-1)))
    return {
        "x": nrm(ks[0], (BATCH, SEQ, D_MODEL), 1.0),
        "p": nrm(ks[1], (DEPTH, BATCH, SEQ, PLE_DIM), 1.0),
        "norm_g": 1.0 + nrm(ks[2], (DEPTH, D_MODEL), 0.02),
        "w_in": nrm(ks[3], (DEPTH, D_MODEL, N_IN), D_MODEL ** -0.5),
        "conv_w": nrm(ks[4], (DEPTH, CONV_K, CONV_CH), CONV_K ** -0.5),
        "conv_b": nrm(ks[5], (DEPTH, CONV_CH), 0.02),
        "dt_bias": dt0 + jnp.log(-jnp.expm1(-dt0)),
        "a_log": jnp.log(jax.random.uniform(ks[7], (DEPTH, SSD_HEADS), f32, minval=1.0, maxval=16.0)),
        "d_skip": 1.0 + nrm(ks[8], (DEPTH, SSD_HEADS), 0.1),
        "ssd_norm_g": 1.0 + nrm(ks[9], (DEPTH, SSD_INNER), 0.02),
        "w_br_ssd": nrm(ks[10], (DEPTH, SSD_INNER, D_MODEL), SSD_INNER ** -0.5),
        "pool_w": nrm(ks[11], (DEPTH, POOL_GROUPS, POOL_GROUP_W, POOL_GROUP_W), POOL_GROUP_W ** -0.5),
        "pool_scale": 1.0 + nrm(ks[12], (DEPTH, POOL_WIDTH), 0.1),
        "w_br_pool": nrm(ks[13], (DEPTH, POOL_WIDTH, D_MODEL), POOL_WIDTH ** -0.5),
        "q_norm_g": 1.0 + nrm(ks[14], (DEPTH, ATTN_HEAD_DIM), 0.02),
        "k_norm_g": 1.0 + nrm(ks[15], (DEPTH, ATTN_HEAD_DIM), 0.02),
        "w_br_attn": nrm(ks[16], (DEPTH, ATTN_OUT, D_MODEL), ATTN_OUT ** -0.5),
        "w_out": nrm(ks[17], (DEPTH, D_MODEL, D_MODEL), D_MODEL ** -0.5),
        "ple_norm_g": 1.0 + nrm(ks[18], (DEPTH, D_MODEL), 0.02),
        "w_ple_gate": nrm(ks[19], (DEPTH, D_MODEL, D_MODEL), D_MODEL ** -0.5),
        "w_ple_proj": nrm(ks[20], (DEPTH, PLE_DIM, D_MODEL), PLE_DIM ** -0.5),
    }


def reference(x, p, norm_g, w_in, conv_w, conv_b, dt_bias, a_log, d_skip, ssd_norm_g, w_br_ssd,
              pool_w, pool_scale, w_br_pool, q_norm_g, k_norm_g, w_br_attn, w_out,
              ple_norm_g, w_ple_gate, w_ple_proj):
    b, seq, _ = x.shape
    for i in range(DEPTH):
        h = _rms_norm(x, norm_g[i])
        proj = jnp.einsum("bld,dn->bln", h, w_in[i])
        (z_ssd, xbc, dt, u_pool, z_pool, q, k, v, z_attn, gate_logits) = jnp.split(proj, IN_SPLITS, axis=-1)
        y_ssd = _ssd_branch(z_ssd, xbc, dt, conv_w[i], conv_b[i], dt_bias[i], a_log[i], d_skip[i], ssd_norm_g[i])
        y_pool = _pool_branch(u_pool, z_pool, pool_w[i], pool_scale[i])
        y_attn = _attn_branch(q, k, v, z_attn, q_norm_g[i], k_norm_g[i])
        gates = jax.nn.sigmoid(gate_logits.reshape(b, seq, N_BRANCHES, D_MODEL))
        merged = (gates[:, :, 0] * (y_ssd @ w_br_ssd[i])
                  + gates[:, :, 1] * (y_pool @ w_br_pool[i])
                  + gates[:, :, 2] * (y_attn @ w_br_attn[i]))
        x = x + merged @ w_out[i]
        ple_gate = jax.nn.sigmoid(_rms_norm(x, ple_norm_g[i]) @ w_ple_gate[i])
        x = x + ple_gate * (p[i] @ w_ple_proj[i])
    return x
```

```python
import numpy as np
from contextlib import ExitStack
import concourse.bass as bass
import concourse.mybir as mybir
from concourse.bass_utils import run_bass_kernel_spmd

F32 = mybir.dt.float32
BF16 = mybir.dt.bfloat16
AF = mybir.ActivationFunctionType
ALU = mybir.AluOpType
AX = mybir.AxisListType

L = 2048
D = 2048
NIN = 20512
DEPTH = 2
EPS = 1e-6
DIL = (1, 4, 16)
LT = 1024
NT = 8
K_ZS, K_XBC, K_UP, K_ZP, K_Q, K_K, K_ZA, K_DT, K_V, K_G = 0, 1024, 2560, 3584, 4608, 5376, 6144, 6400, 6416, 7184
NCOL = 13328
PAIRS = [[0, 1], [2, 3], [4, 5], [6, 7]]


class Reg:
    __slots__ = ("name", "w", "rs")

    def __init__(self, name):
        self.name = name
        self.w = None
        self.rs = []


class DSem:
    __slots__ = ("sem", "count", "store")

    def __init__(self, sem, store):
        self.sem = sem
        self.count = 0
        self.store = store


class K:
    def __init__(self, nc, es):
        self.nc = nc
        self.es = es
        self.eng = {"pe": nc.tensor, "act": nc.scalar, "dve": nc.vector, "pool": nc.gpsimd, "sp": nc.sync}
        self.esem = {e: es.enter_context(nc.semaphore("s_" + e)) for e in self.eng}
        self.ecnt = {e: 0 for e in self.eng}
        self.waited = {e: {} for e in self.eng}
        self.regs = {}
        self.dsems = {}
        self.dsem_by_id = {}
        self.semobj = {}
        for e, s in self.esem.items():
            self.semobj[id(s)] = s
        self.rr = {}

    def R(self, *key):
        r = self.regs.get(key)
        if r is None:
            r = Reg(key)
            self.regs[key] = r
        return r

    def nxt(self, name, n):
        v = self.rr.get(name, 0)
        self.rr[name] = v + 1
        return v % n

    def dsem(self, name, store=False):
        d = self.dsems.get(name)
        if d is None:
            sem = self.es.enter_context(self.nc.semaphore("d_" + name))
            d = DSem(sem, store)
            self.dsems[name] = d
            self.dsem_by_id[id(sem)] = d
            self.semobj[id(sem)] = sem
        return d

    def _wait(self, e, toks):
        need = {}
        for (sem, val) in toks:
            d = self.dsem_by_id.get(id(sem))
            if d is not None and not d.store:
                val = d.count
            if need.get(id(sem), 0) < val:
                need[id(sem)] = val
        for sid, val in need.items():
            if self.waited[e].get(sid, 0) < val:
                self.eng[e].wait_ge(self.semobj[sid], val)
                self.waited[e][sid] = val

    def _deps(self, e, reads, writes):
        own = id(self.esem[e]) if e in self.esem else None
        toks = []
        for r in reads:
            if r.w is not None:
                if not (e == "pe" and id(r.w[0]) == own):
                    toks.append(r.w)
        for w in writes:
            if w.w is not None and not (e == "pe" and id(w.w[0]) == own):
                toks.append(w.w)
            for t in w.rs:
                if not (e == "pe" and id(t[0]) == own):
                    toks.append(t)
        return toks

    def op(self, e, fn, reads=(), writes=(), inc=True):
        writes = list(writes) + [r for r in reads if r.name[0] == "ps"]
        self._wait(e, self._deps(e, reads, writes))
        ins = fn(self.eng[e])
        if inc:
            self.ecnt[e] += 1
            ins.then_inc(self.esem[e], 1)
            tok = (self.esem[e], self.ecnt[e])
        else:
            tok = (self.esem[e], self.ecnt[e] + 1)
        for w in writes:
            w.w = tok
            w.rs = []
        for r in reads:
            r.rs.append(tok)
            if len(r.rs) > 64:
                r.rs = r.rs[-48:]
        return ins

    def dma(self, q, out, in_, reads=(), writes=(), sem=None, store=False, **kw):
        d = self.dsem(sem, store)
        self._wait(q, self._deps(q, reads, writes))
        d.count += 16
        self.eng[q].dma_start(out=out, in_=in_, **kw).then_inc(d.sem, 16)
        tok = (d.sem, d.count)
        for w in writes:
            w.w = tok
            w.rs = []
        for r in reads:
            r.rs.append(tok)

    def collective(self, kind, in_ap, out_ap, groups, name):
        d = self.dsem("cc_" + name, True)
        self.fence(("pool",))
        d.count += 1
        self.eng["pool"].collective_compute(kind, ALU.bypass, replica_groups=groups, ins=[in_ap], outs=[out_ap]).then_inc(d.sem, 1)

    def fence(self, engines=("sp", "pool")):
        for e in engines:
            toks = [(d.sem, d.count) for d in self.dsems.values() if d.store and d.count > 0]
            self._wait(e, toks)

    def barrier(self):
        toks = [(self.esem[e], self.ecnt[e]) for e in self.eng if self.ecnt[e] > 0]
        toks += [(d.sem, d.count) for d in self.dsems.values() if d.count > 0]
        for e in self.eng:
            self._wait(e, toks)
        for r in self.regs.values():
            r.w = None
            r.rs = []


def _consts():
    c = np.zeros((128, 640), np.float32)
    i = np.arange(128)
    c[:, 0:128] = np.eye(128, dtype=np.float32)
    c[:, 128:256] = (i[:, None] <= i[None, :])
    c[:, 256:384] = (i[:, None] >= i[None, :])
    c[:, 384:512] = (i[:, None] > i[None, :])
    c[:, 512:640] = 1.0
    return c


CO_ID, CO_LE, CO_GE, CO_GT, CO_ONE, CO_INVC = 0, 128, 256, 384, 512, 640


class Ctx:
    pass


def sbt(k, es, name, shape, dt):
    return es.enter_context(k.nc.sbuf_tensor(name + getattr(k, "sfx", ""), shape, dt))


def load_consts(k, es, g, consts_ap):
    nc = k.nc
    g.cf = sbt(k, es, "cf", [128, 640], F32)
    g.cb = sbt(k, es, "cb", [128, 640], BF16)
    k.dma("sp", g.cf[:], consts_ap, writes=[k.R("cf")], sem="cf")
    k.op("dve", lambda e: e.tensor_copy(out=g.cb[:], in_=g.cf[:]), reads=[k.R("cf")], writes=[k.R("cb")])
    g.ps = es.enter_context(nc.psum_tensor("ps", [128, 4096], F32))
    g.ident = g.cf[:, CO_ID:CO_ID + 128]


def psb(g, b, n=512, off=0):
    return g.ps[:, b * 512 + off: b * 512 + off + n]


def load_cols(k, g, es, name, rows_ap, nrows, dst):
    tmp = sbt(k, es, "lc_" + name, [128, 128], F32)
    k.op("dve", lambda e: e.memset(tmp[:], 0.0), writes=[k.R("lc", name)])
    k.dma("sp", tmp[0:nrows, :], rows_ap, writes=[k.R("lc", name)], sem="lc_" + name)
    b = k.nxt("bank", 8)
    k.op("pe", lambda e: e.transpose(out=psb(g, b, 128), in_=tmp[:], identity=g.ident),
         reads=[k.R("lc", name), k.R("cf")], writes=[k.R("ps", b)])
    k.op("dve", lambda e: e.tensor_copy(out=dst, in_=psb(g, b, nrows)), reads=[k.R("ps", b)], writes=[k.R("vec", name)])


def wload(k, g, Wl, c0, cw):
    s = k.nxt("wslot", 2)
    k.dma("pool", g.wA[s][:, :, 0:cw], Wl[:, c0:c0 + cw].rearrange("(kt p) c -> p kt c", p=128),
          writes=[k.R("wA", s)], sem="wA%d" % s)
    return s


def phase_A(k, g, li, x_src, P, S):
    nc = k.nc
    with ExitStack() as es:
        g.wA = [sbt(k, es, "wA%d" % i, [128, 16, 512], BF16) for i in range(2)]
        Wl = P["w_in"][li]
        es0 = ExitStack()
        hTo = sbt(k, es0, "hTo", [128, 16, LT], BF16)
        gB = sbt(k, es0, "gB", [128, D], F32)
        xt = [sbt(k, es0, "xt%d" % i, [128, D], F32) for i in range(2)]
        xn = [sbt(k, es0, "xn%d" % i, [128, D], F32) for i in range(2)]
        junk = sbt(k, es0, "junk", [128, D], BF16)
        stat = sbt(k, es0, "stat", [128, 48], F32)
        stq = [sbt(k, es0, "stq%d" % i, [128, 512], F32) for i in range(2)]
        k.dma("sp", gB[:], P["norm_g"][li].partition_broadcast(128), writes=[k.R("gB")], sem="gB")
        for tt in range(NT):
            s = tt % 2
            k.dma("sp", xt[s][:], x_src[tt * 128:(tt + 1) * 128, :], writes=[k.R("xt", s)], sem="xt%d" % s)
            k.op("act", lambda e: e.activation(out=junk[:], in_=xt[s][:], func=AF.Square, accum_out=stat[:, tt:tt + 1]),
                 reads=[k.R("xt", s)], writes=[k.R("junk"), k.R("st", tt)])
            k.op("act", lambda e: e.activation(out=stat[:, 16 + tt:17 + tt], in_=stat[:, tt:tt + 1], func=AF.Sqrt,
                                               scale=1.0 / D, bias=g.epsb[:, 0:1]),
                 reads=[k.R("st", tt), k.R("epsb")], writes=[k.R("st2", tt)])
            k.op("dve", lambda e: e.reciprocal(out=stat[:, 32 + tt:33 + tt], in_=stat[:, 16 + tt:17 + tt]),
                 reads=[k.R("st2", tt)], writes=[k.R("st3", tt)])
            k.op("dve", lambda e: e.scalar_tensor_tensor(out=xn[s][:], in0=xt[s][:], scalar=stat[:, 32 + tt:33 + tt],
                                                         in1=gB[:], op0=ALU.mult, op1=ALU.mult),
                 reads=[k.R("xt", s), k.R("st3", tt), k.R("gB")], writes=[k.R("xn", s)])
            for b4 in range(4):
                b = k.nxt("bank", 8)
                for j in range(4):
                    kt = b4 * 4 + j
                    k.op("pe", lambda e: e.transpose(out=psb(g, b, 128, j * 128), in_=xn[s][:, kt * 128:(kt + 1) * 128],
                                                     identity=g.ident),
                         reads=[k.R("xn", s), k.R("cf")], writes=[k.R("ps", b)], inc=(j == 3))
                dst = hTo[:, b4 * 4:(b4 + 1) * 4, tt * 128:(tt + 1) * 128]
                src = psb(g, b).rearrange("p (a c) -> p a c", a=4)
                if b4 % 2 == 0:
                    k.op("act", lambda e: e.copy(out=dst, in_=src), reads=[k.R("ps", b)], writes=[k.R("hTo", tt)])
                else:
                    k.op("dve", lambda e: e.tensor_copy(out=dst, in_=src), reads=[k.R("ps", b)], writes=[k.R("hTo", tt)])
        hTo_all = [k.R("hTo", tt) for tt in range(NT)]
        for c in range(2):
            k.dma("sp", S["hTb%d" % c].rearrange("(kt p) t -> p kt t", p=128), hTo[:, c * 8:(c + 1) * 8, :], reads=hTo_all, sem="hTb", store=True)
        for c in range(2):
            k.collective("AllGather", S["hTb%d" % c], S["hTg%d" % c], g.pairs, "h")
        for cg in range(12):
            s = wload(k, g, Wl, K_G + cg * 512, 512)
            for tt in range(NT):
                b = k.nxt("bank", 8)
                for kt in range(16):
                    k.op("pe", lambda e: e.matmul(out=psb(g, b), lhsT=hTo[:, kt, tt * 128:(tt + 1) * 128], rhs=g.wA[s][:, kt, :],
                                                  start=(kt == 0), stop=(kt == 15)),
                         reads=[k.R("wA", s), k.R("hTo", tt)], writes=[k.R("ps", b)], inc=(kt == 15))
                s2 = k.nxt("stq", 2)
                k.op("act", lambda e: e.activation(out=stq[s2][:], in_=psb(g, b), func=AF.Sigmoid), reads=[k.R("ps", b)], writes=[k.R("stq", s2)])
                k.dma("sp", S["gates"][tt * 128:(tt + 1) * 128, cg * 512:(cg + 1) * 512], stq[s2][:], reads=[k.R("stq", s2)],
                      sem="stq%d" % s2, store=True)
        k.barrier()
        es0.close()
        hT = sbt(k, es, "hT", [128, 16, L], BF16)
        k.fence(("sp",))
        for q in range(2):
            for c in range(2):
                k.dma("sp", hT[:, c * 8:(c + 1) * 8, q * LT:(q + 1) * LT],
                      S["hTg%d" % c][q * 1024:(q + 1) * 1024, :].rearrange("(kt p) t -> p kt t", p=128), writes=[k.R("hT")], sem="hTl")
        hT_all = [k.R("hT")]
        Wk = [sbt(k, es, "Wk%d" % i, [128, L + 16], F32) for i in range(2)]

        pend_cm = []

        def proj_cm(c0, ntiles, epi):
            t = 0
            while t < ntiles:
                nt = min(4, ntiles - t)
                s = wload(k, g, Wl, c0 + t * 128, nt * 128)
                for ct in range(nt):
                    half = k.nxt("half", 2)
                    for tg in range(4):
                        b = half * 4 + tg
                        for kt in range(16):
                            k.op("pe", lambda e: e.matmul(out=psb(g, b), lhsT=g.wA[s][:, kt, ct * 128:(ct + 1) * 128],
                                                          rhs=hT[:, kt, tg * 512:(tg + 1) * 512], start=(kt == 0), stop=(kt == 15)),
                                 reads=[k.R("wA", s)] + hT_all, writes=[k.R("ps", b)], inc=(kt == 15))
                    if pend_cm:
                        pend_cm.pop(0)()
                    dfr = epi(t + ct, g.ps[:, half * 2048:(half + 1) * 2048], [k.R("ps", half * 4 + i) for i in range(4)], half)
                    if dfr is not None:
                        pend_cm.append(dfr)
                t += nt
            while pend_cm:
                pend_cm.pop(0)()

        stg = [sbt(k, es, "stg%d" % i, [128, L], F32) for i in range(2)]

        def epi_silu(dst, func=AF.Silu):
            def f(i, pa, pr, half):
                s = k.nxt("stg", 2)
                k.op("act", lambda e: e.activation(out=stg[s][:], in_=pa, func=func), reads=pr, writes=[k.R("stg", s)])
                k.dma("sp", dst[i * 128:(i + 1) * 128, :], stg[s][:], reads=[k.R("stg", s)], sem="stg%d" % s, store=True)
            return f

        proj_cm(K_ZS, 8, epi_silu(S["zsT"]))
        proj_cm(K_ZP, 8, epi_silu(S["zpT"]))
        proj_cm(K_ZA, 2, epi_silu(S["zaT"]))

        cwv = sbt(k, es, "cwv", [128, 48], F32)
        cbv = sbt(k, es, "cbv", [128, 12], F32)
        load_cols(k, g, es, "cw", P["conv_w"][li].rearrange("k (i p) -> (k i) p", p=128), 48, cwv[:])
        load_cols(k, g, es, "cbias", P["conv_b"][li].rearrange("(i p) -> i p", p=128), 12, cbv[:])
        xp = [sbt(k, es, "xp%d" % i, [128, L + 16], F32) for i in range(2)]
        acc = Wk
        for i in range(2):
            k.op("pool", lambda e: e.memset(xp[i][:, 0:16], 0.0), writes=[k.R("xp", i)])

        def epi_conv(i, pa, pr, half):
            s = k.nxt("xp", 2)
            k.op("act", lambda e: e.copy(out=xp[s][:, 16:16 + L], in_=pa), reads=pr, writes=[k.R("xp", s)])
            k.op("act", lambda e: e.activation(out=acc[s][:, 0:L], in_=pa, func=AF.Copy, scale=cwv[:, 3 * 12 + i:3 * 12 + i + 1]),
                 reads=pr + [k.R("vec", "cw")], writes=[k.R("Wk", s)])
            for kk in (2, 1, 0):
                sh = 3 - kk
                k.op("dve", lambda e: e.scalar_tensor_tensor(out=acc[s][:, 0:L], in0=xp[s][:, 16 - sh:16 - sh + L],
                                                             scalar=cwv[:, kk * 12 + i:kk * 12 + i + 1], in1=acc[s][:, 0:L],
                                                             op0=ALU.mult, op1=ALU.add),
                     reads=[k.R("xp", s), k.R("Wk", s)], writes=[k.R("Wk", s)])
            s2 = k.nxt("stg", 2)
            k.op("act", lambda e: e.activation(out=stg[s2][:], in_=acc[s][:, 0:L], func=AF.Silu, bias=cbv[:, i:i + 1]),
                 reads=[k.R("Wk", s), k.R("vec", "cbias")], writes=[k.R("stg", s2)])
            k.dma("sp", S["xbcT"][i * 128:(i + 1) * 128, :], stg[s2][:], reads=[k.R("stg", s2)], sem="stg%d" % s2, store=True)

        proj_cm(K_XBC, 12, epi_conv)

        bufA, bufB = Wk
        pacc = sbt(k, es, "pacc", [128, L], F32)
        stgb = [sbt(k, es, "stgb%d" % i, [128, L], BF16) for i in range(2)]
        tmp16 = sbt(k, es, "tmp16", [128, 16], F32)
        acc16 = sbt(k, es, "acc16", [128, 16], F32)
        k.op("pool", lambda e: e.memset(bufA[:, 0:16], 0.0), writes=[k.R("Wk", 0)])
        k.op("pool", lambda e: e.memset(bufB[:, 0:16], 0.0), writes=[k.R("Wk", 1)])
        pc = g.poolc

        def epi_pool(i, pa, pr, half):
            gl = i // 4
            s = k.nxt("xp", 2)
            k.op("act", lambda e: e.copy(out=xp[s][:, 16:16 + L], in_=pa), reads=pr, writes=[k.R("xp", s)])
            src, srcR = xp[s], k.R("xp", s)
            bufs = [(bufA, k.R("Wk", 0)), (bufB, k.R("Wk", 1))]
            cand = (0, 2) if gl == 0 else (1, 3)
            first = True
            for j in range(cand[1] + 1):
                sh = 1 << j
                dst, dstR = bufs[j % 2]
                k.op("pool", lambda e: e.tensor_tensor(out=dst[:, 16:16 + L], in0=src[:, 16:16 + L], in1=src[:, 16 - sh:16 - sh + L],
                                                       op=ALU.add),
                     reads=[srcR], writes=[dstR])
                src, srcR = dst, dstR
                if j not in cand:
                    continue
                cj = pc[:, gl * 4 + j:gl * 4 + j + 1]
                rowj = pc[:, 8 + (gl * 4 + j) * 16: 8 + (gl * 4 + j) * 16 + 16]
                if first:
                    first = False
                    k.op("act", lambda e: e.activation(out=pacc[:], in_=src[:, 16:16 + L], func=AF.Copy, scale=cj),
                         reads=[srcR, k.R("poolc")], writes=[k.R("pacc")])
                    k.op("dve", lambda e: e.tensor_tensor(out=acc16[:], in0=src[:, 16:32], in1=rowj, op=ALU.mult),
                         reads=[srcR, k.R("poolc")], writes=[k.R("acc16")])
                else:
                    k.op("dve", lambda e: e.scalar_tensor_tensor(out=pacc[:], in0=src[:, 16:16 + L], scalar=cj, in1=pacc[:],
                                                                 op0=ALU.mult, op1=ALU.add),
                         reads=[srcR, k.R("poolc"), k.R("pacc")], writes=[k.R("pacc")])
                    k.op("dve", lambda e: e.tensor_tensor(out=tmp16[:], in0=src[:, 16:32], in1=rowj, op=ALU.mult),
                         reads=[srcR, k.R("poolc")], writes=[k.R("tmp16")])
                    k.op("dve", lambda e: e.tensor_tensor(out=acc16[:], in0=acc16[:], in1=tmp16[:], op=ALU.add),
                         reads=[k.R("tmp16"), k.R("acc16")], writes=[k.R("acc16")])
            sb_ = k.nxt("stgb", 2)
            k.op("dve", lambda e: e.tensor_tensor(out=stgb[sb_][:], in0=pacc[:], in1=xp[s][:, 16:16 + L], op=ALU.subtract),
                 reads=[k.R("pacc"), k.R("xp", s)], writes=[k.R("stgb", sb_)])
            k.op("dve", lambda e: e.tensor_tensor(out=stgb[sb_][:, 0:16], in0=acc16[:], in1=xp[s][:, 16:32], op=ALU.subtract),
                 reads=[k.R("acc16"), k.R("xp", s), k.R("stgb", sb_)], writes=[k.R("stgb", sb_)])
            k.dma("sp", S["poolT"][i * 128:(i + 1) * 128, :], stgb[sb_][:], reads=[k.R("stgb", sb_)], sem="stgb%d" % sb_, store=True)

        proj_cm(K_UP, 8, epi_pool)

        gq = sbt(k, es, "gq", [128, 2], F32)
        load_cols(k, g, es, "gqk", P["qk_g"][li], 2, gq[:])
        raw, rt = Wk[0][:, 0:L], Wk[1][:, 0:L]
        sqb = sbt(k, es, "sqb", [128, L], BF16)

        sqbs = [sqb, sbt(k, es, "sqb2", [128, L], BF16)]

        def epi_qk(i, pa, pr, half):
            isk = i // 6
            hl = i % 6
            d = DIL[hl // 2]
            sq = sqbs[i % 2]
            k.op("act", lambda e: e.activation(out=sq[:], in_=pa, func=AF.Square), reads=pr, writes=[k.R("sqb", i % 2)])
            k.op("act", lambda e: e.copy(out=raw, in_=pa), reads=pr, writes=[k.R("Wk", 0)])

            def deferred():
                for tg in range(4):
                    b = half * 4 + tg
                    k.op("pe", lambda e: e.matmul(out=psb(g, b), lhsT=g.cb[:, CO_ONE:CO_ONE + 128], rhs=sq[:, tg * 512:(tg + 1) * 512],
                                                  start=True, stop=True),
                         reads=[k.R("sqb", i % 2), k.R("cb")], writes=[k.R("ps", b)])
                k.op("act", lambda e: e.activation(out=rt, in_=pa, func=AF.Sqrt, scale=1.0 / 128, bias=g.epsb[:, 0:1]),
                     reads=pr + [k.R("epsb")], writes=[k.R("Wk", 1)])
                k.op("dve", lambda e: e.reciprocal(out=rt, in_=rt), reads=[k.R("Wk", 1)], writes=[k.R("Wk", 1)])
                sb_ = k.nxt("stgb", 2)
                outv = stgb[sb_][:].rearrange("p (r u) -> p u r", r=d)
                in0 = raw.rearrange("p (u r) -> p u r", r=d)
                in1 = rt.rearrange("p (u r) -> p u r", r=d)
                k.op("dve", lambda e: e.scalar_tensor_tensor(out=outv, in0=in0, scalar=gq[:, isk:isk + 1], in1=in1, op0=ALU.mult, op1=ALU.mult),
                     reads=[k.R("Wk", 0), k.R("Wk", 1), k.R("vec", "gqk")], writes=[k.R("stgb", sb_)])
                dstT = S["knT"] if isk else S["qnT"]
                k.dma("sp", dstT[hl * 128:(hl + 1) * 128, :], stgb[sb_][:], reads=[k.R("stgb", sb_)], sem="stgb%d" % sb_, store=True)
            return deferred

        proj_cm(K_Q, 12, epi_qk)

        def proj_tm(c0, cw, tokens_of, epi):
            s = wload(k, g, Wl, c0, cw)
            for tt in range(16):
                b = k.nxt("bank", 8)
                sl = tokens_of(tt)
                for kt in range(16):
                    k.op("pe", lambda e: e.matmul(out=psb(g, b, cw), lhsT=hT[:, kt, sl], rhs=g.wA[s][:, kt, 0:cw],
                                                  start=(kt == 0), stop=(kt == 15)),
                         reads=[k.R("wA", s)] + hT_all, writes=[k.R("ps", b)], inc=(kt == 15))
                epi(tt, b)

        nat = lambda tt: slice(tt * 128, (tt + 1) * 128)
        stqb = [sbt(k, es, "stqb%d" % i, [128, 256], BF16) for i in range(2)]
        for gi in range(3):
            d = DIL[gi]
            nb = 16 // d

            def tok(tt, d=d, nb=nb):
                r, jb = tt // nb, tt % nb
                st = r + d * jb * 128
                return slice(st, st + d * 127 + 1, d)

            def epi_v(tt, b, gi=gi):
                s = k.nxt("stqb", 2)
                k.op("act", lambda e: e.copy(out=stqb[s][:], in_=psb(g, b, 256)), reads=[k.R("ps", b)], writes=[k.R("stqb", s)])
                k.dma("sp", S["vTM"][gi, tt * 128:(tt + 1) * 128, :], stqb[s][:], reads=[k.R("stqb", s)], sem="stqb%d" % s, store=True)
            proj_tm(K_V + gi * 256, 256, tok, epi_v)

        dtb = sbt(k, es, "dtb", [128, 16], F32)
        k.dma("sp", dtb[:], P["dt_bias"][li].partition_broadcast(128), writes=[k.R("dtb")], sem="dtb")
        dtall = sbt(k, es, "dtall", [128, 16, 16], F32)
        dtt = sbt(k, es, "dtt", [128, 16], F32)

        def epi_dt(tt, b):
            k.op("dve", lambda e: e.tensor_tensor(out=dtt[:], in0=psb(g, b, 16), in1=dtb[:], op=ALU.add),
                 reads=[k.R("ps", b), k.R("dtb")], writes=[k.R("dtt")])
            k.op("act", lambda e: e.activation(out=dtt[:], in_=dtt[:], func=AF.Exp), reads=[k.R("dtt")], writes=[k.R("dtt")])
            k.op("act", lambda e: e.activation(out=dtall[:, tt, :], in_=dtt[:], func=AF.Ln, bias=g.oneb[:, 0:1]),
                 reads=[k.R("dtt"), k.R("epsb")], writes=[k.R("dtall")])
        proj_tm(K_DT, 16, nat, epi_dt)
        k.dma("sp", S["dtv"].rearrange("(tt p) h -> p tt h", p=128), dtall[:], reads=[k.R("dtall")], sem="dtall", store=True)
    k.barrier()


def ys_tile(S, i):
    return S["ysend%d" % (i // 3)][(i % 3) * 128:(i % 3 + 1) * 128, :]


def alloc_scratch(nc, kind="Internal"):
    S = {}

    def dt_(name, shape, dt):
        S[name] = nc.dram_tensor(name, shape, dt, kind=kind).ap()
    for c in range(2):
        dt_("hTb%d" % c, [1024, LT], BF16)
        dt_("hTg%d" % c, [2 * 1024, LT], BF16)
    dt_("zsT", [1024, L], F32)
    dt_("zpT", [1024, L], F32)
    dt_("zaT", [256, L], F32)
    dt_("xbcT", [1536, L], F32)
    dt_("poolT", [1024, L], BF16)
    dt_("qnT", [768, L], BF16)
    dt_("knT", [768, L], BF16)
    dt_("gates", [LT, 6144], F32)
    dt_("vTM", [3, L, 256], BF16)
    dt_("dtv", [L, 16], F32)
    for c in range(6):
        dt_("ysend%d" % c, [384, L], BF16)
        dt_("yg%d" % c, [2 * 384, L], BF16)
    dt_("ssp", [16, 128], F32)
    dt_("ssg", [32, 128], F32)
    dt_("xmid", [LT, D], F32)
    dt_("x1", [LT, D], F32)
    return S


PARAM_SHAPES = {
    "norm_g": [DEPTH, D], "w_in": [DEPTH, D, NCOL], "conv_w": [DEPTH, 4, 1536], "conv_b": [DEPTH, 1536],
    "dt_bias": [DEPTH, 16], "a_log": [DEPTH, 16], "d_skip": [DEPTH, 16], "ssd_norm_g": [DEPTH, 1024],
    "w_br": [DEPTH, 4608, D], "pool_w": [DEPTH, 2, 512, 512], "pool_scale": [DEPTH, 1024],
    "qk_g": [DEPTH, 2, 128], "w_out": [DEPTH, D, D], "ple_norm_g": [DEPTH, D], "w_ple_gate": [DEPTH, D, D],
    "w_ple_proj": [DEPTH, 256, D],
}


def setup_globals(k, es, g, consts_ap, sel_ap):
    load_consts(k, es, g, consts_ap)
    g.epsb = sbt(k, es, "epsb", [128, 1], F32)
    g.oneb = sbt(k, es, "oneb", [128, 1], F32)
    k.op("pool", lambda e: e.memset(g.epsb[:], EPS), writes=[k.R("epsb")])
    k.op("pool", lambda e: e.memset(g.oneb[:], 1.0), writes=[k.R("epsb")])
    g.poolc = sbt(k, es, "poolc", [128, 144], F32)
    k.dma("sp", g.poolc[:], sel_ap, writes=[k.R("poolc")], sem="poolc")


def phase_P(k, g, li, P, S):
    with ExitStack() as es:
        wP = sbt(k, es, "wP", [128, 4, 512], BF16)
        pl = sbt(k, es, "pl", [128, 4, L], BF16)
        zp = [sbt(k, es, "zp%d" % i, [128, L], F32) for i in range(2)]
        psc = sbt(k, es, "psc", [128, 8], F32)
        stgb = [sbt(k, es, "pstgb%d" % i, [128, L], BF16) for i in range(2)]
        load_cols(k, g, es, "psc", P["pool_scale"][li].rearrange("(i p) -> i p", p=128), 8, psc[:])
        for gi in range(2):
            k.dma("pool", wP[:], P["pool_w"][li, gi].rearrange("(kt p) c -> p kt c", p=128), writes=[k.R("wP")], sem="wP")
            k.dma("sp", pl[:], S["poolT"][gi * 512:(gi + 1) * 512, :].rearrange("(kt p) t -> p kt t", p=128),
                  writes=[k.R("pl")], sem="pl")
            for dt_ in range(4):
                i = gi * 4 + dt_
                s = k.nxt("zp", 2)
                k.dma("sp", zp[s][:], S["zpT"][i * 128:(i + 1) * 128, :], writes=[k.R("zp", s)], sem="zp%d" % s)
                half = k.nxt("half", 2)
                for tg in range(4):
                    b = half * 4 + tg
                    for kt in range(4):
                        k.op("pe", lambda e: e.matmul(out=psb(g, b), lhsT=wP[:, kt, dt_ * 128:(dt_ + 1) * 128],
                                                      rhs=pl[:, kt, tg * 512:(tg + 1) * 512], start=(kt == 0), stop=(kt == 3)),
                             reads=[k.R("wP"), k.R("pl")], writes=[k.R("ps", b)], inc=(kt == 3))
                pa = g.ps[:, half * 2048:(half + 1) * 2048]
                pr = [k.R("ps", half * 4 + t) for t in range(4)]
                sb_ = k.nxt("pstgb", 2)
                k.op("dve", lambda e: e.scalar_tensor_tensor(out=stgb[sb_][:], in0=pa, scalar=psc[:, i:i + 1], in1=zp[s][:],
                                                             op0=ALU.mult, op1=ALU.mult),
                     reads=pr + [k.R("vec", "psc"), k.R("zp", s)], writes=[k.R("pstgb", sb_)])
                k.dma("sp", ys_tile(S, 8 + i), stgb[sb_][:], reads=[k.R("pstgb", sb_)], sem="pstgb%d" % sb_, store=True)
    k.barrier()


def phase_AT(k, g, li, P, S):
    with ExitStack() as es:
        gqb = sbt(k, es, "gqb", [128, 256], F32)
        mx = sbt(k, es, "mx", [128, 4], F32)
        k.dma("sp", gqb[:], P["qk_g"][li].rearrange("a c -> (a c)").partition_broadcast(128), writes=[k.R("gqb")], sem="gqb")
        gq2 = sbt(k, es, "gq2", [128, 256], F32)
        k.op("dve", lambda e: e.tensor_scalar(out=gq2[:], in0=gqb[:], scalar1=-1.0, scalar2=None, op0=ALU.mult), reads=[k.R("gqb")], writes=[k.R("gq2")])
        k.op("dve", lambda e: e.tensor_tensor(out=gqb[:], in0=gqb[:], in1=gq2[:], op=ALU.max), reads=[k.R("gqb"), k.R("gq2")], writes=[k.R("gqb")])
        k.op("dve", lambda e: e.tensor_reduce(out=mx[:, 0:1], in_=gqb[:, 0:128], axis=AX.X, op=ALU.max), reads=[k.R("gqb")], writes=[k.R("mx")])
        k.op("dve", lambda e: e.tensor_reduce(out=mx[:, 1:2], in_=gqb[:, 128:256], axis=AX.X, op=ALU.max), reads=[k.R("gqb"), k.R("mx")], writes=[k.R("mx")])
        k.op("dve", lambda e: e.tensor_tensor(out=mx[:, 2:3], in0=mx[:, 0:1], in1=mx[:, 1:2], op=ALU.mult), reads=[k.R("mx")], writes=[k.R("mx")])
        k.op("dve", lambda e: e.tensor_scalar(out=mx[:, 3:4], in0=mx[:, 2:3], scalar1=-float(np.sqrt(128.0)), scalar2=None, op0=ALU.mult),
             reads=[k.R("mx")], writes=[k.R("mx")])
        negM = mx[:, 3:4]
        qn = [sbt(k, es, "qn%d" % i, [128, L], BF16) for i in range(3)]
        kn = [sbt(k, es, "kn%d" % i, [128, L], BF16) for i in range(3)]
        vt = [sbt(k, es, "vt%d" % i, [128, 16, 128], BF16) for i in range(3)]
        acc = sbt(k, es, "aacc", [128, 2, L], F32)
        E = [sbt(k, es, "E%d" % i, [128, 256], BF16) for i in range(6)]
        za = sbt(k, es, "za", [128, L], F32)
        rec = sbt(k, es, "rec", [128, L], F32)
        yb = sbt(k, es, "yb", [128, L], BF16)
        sc = float(1.0 / np.sqrt(128.0))
        for j in range(2):
            for gi in range(3):
                h = gi * 2 + j
                k.dma("sp", qn[gi][:], S["qnT"][h * 128:(h + 1) * 128, :], writes=[k.R("qn", gi)], sem="qn%d" % gi)
                k.dma("sp", kn[gi][:], S["knT"][h * 128:(h + 1) * 128, :], writes=[k.R("kn", gi)], sem="kn%d" % gi)
                k.dma("sp", vt[gi][:], S["vTM"][gi, :, j * 128:(j + 1) * 128].rearrange("(blk p) c -> p blk c", p=128),
                      writes=[k.R("vt", gi)], sem="vt%d" % gi)
            k.dma("sp", za[:], S["zaT"][j * 128:(j + 1) * 128, :], writes=[k.R("za")], sem="za")
            for gi in range(3):
                d = DIL[gi]
                nb = 16 // d
                rd = [k.R("qn", gi), k.R("kn", gi)]
                pendA = []

                def finishA(item):
                    blk, e_, hp, gi_, d_, nb_ = item
                    r, jb = blk // nb_, blk % nb_
                    b2 = k.nxt("bank", 8)
                    rv = [k.R("vt", gi_), k.R("E", e_), k.R("cb")]
                    k.op("pe", lambda e: e.matmul(out=psb(g, b2, 128, 0), lhsT=vt[gi_][:, blk, :], rhs=E[e_][:, 0:128], start=True, stop=not hp),
                         reads=rv, writes=[k.R("ps", b2)], inc=False)
                    if hp:
                        k.op("pe", lambda e: e.matmul(out=psb(g, b2, 128, 0), lhsT=vt[gi_][:, blk - 1, :], rhs=E[e_][:, 128:256], start=False, stop=True),
                             reads=rv, writes=[k.R("ps", b2)], inc=False)
                    k.op("pe", lambda e: e.matmul(out=psb(g, b2, 128, 128), lhsT=g.cb[:, CO_ONE:CO_ONE + 128], rhs=E[e_][:, 0:128], start=True, stop=not hp),
                         reads=rv, writes=[k.R("ps", b2)], inc=not hp)
                    if hp:
                        k.op("pe", lambda e: e.matmul(out=psb(g, b2, 128, 128), lhsT=g.cb[:, CO_ONE:CO_ONE + 128], rhs=E[e_][:, 128:256], start=False, stop=True),
                             reads=rv, writes=[k.R("ps", b2)])
                    st = r + d_ * jb * 128
                    accv = acc[:, :, st:st + d_ * 127 + 1:d_]
                    psv = psb(g, b2, 256).rearrange("p (a c) -> p a c", a=2)
                    if gi_ == 0:
                        k.op("act", lambda e: e.copy(out=accv, in_=psv), reads=[k.R("ps", b2)], writes=[k.R("aacc")])
                    else:
                        k.op("dve", lambda e: e.tensor_tensor(out=accv, in0=accv, in1=psv, op=ALU.add), reads=[k.R("ps", b2), k.R("aacc")], writes=[k.R("aacc")])
                for blk in range(16):
                    r, jb = blk // nb, blk % nb
                    hp = jb > 0
                    wd = 256 if hp else 128
                    b = k.nxt("bank", 8)
                    qs = qn[gi][:, blk * 128:(blk + 1) * 128]
                    k.op("pe", lambda e: e.matmul(out=psb(g, b, 128, 0), lhsT=kn[gi][:, blk * 128:(blk + 1) * 128], rhs=qs, start=True, stop=True),
                         reads=rd, writes=[k.R("ps", b)], inc=not hp)
                    if hp:
                        k.op("pe", lambda e: e.matmul(out=psb(g, b, 128, 128), lhsT=kn[gi][:, (blk - 1) * 128:blk * 128], rhs=qs, start=True, stop=True),
                             reads=rd, writes=[k.R("ps", b)])
                    e_ = k.nxt("E", 6)
                    k.op("act", lambda e: e.activation(out=E[e_][:, 0:wd], in_=psb(g, b, wd), func=AF.Exp, scale=sc, bias=negM),
                         reads=[k.R("ps", b), k.R("mx")], writes=[k.R("E", e_)])
                    k.op("dve", lambda e: e.tensor_tensor(out=E[e_][:, 0:wd], in0=E[e_][:, 0:wd], in1=g.cb[:, CO_LE:CO_LE + wd], op=ALU.mult),
                         reads=[k.R("E", e_), k.R("cb")], writes=[k.R("E", e_)])
                    pendA.append((blk, e_, hp, gi, d, nb))
                    if len(pendA) > 3:
                        finishA(pendA.pop(0))
                while pendA:
                    finishA(pendA.pop(0))
            k.op("dve", lambda e: e.reciprocal(out=rec[:], in_=acc[:, 1, :]), reads=[k.R("aacc")], writes=[k.R("rec")])
            k.op("dve", lambda e: e.tensor_tensor(out=rec[:], in0=rec[:], in1=acc[:, 0, :], op=ALU.mult), reads=[k.R("aacc"), k.R("rec")], writes=[k.R("rec")])
            k.op("dve", lambda e: e.tensor_tensor(out=yb[:], in0=rec[:], in1=za[:], op=ALU.mult), reads=[k.R("rec"), k.R("za")], writes=[k.R("yb")])
            k.dma("sp", ys_tile(S, 16 + j), yb[:], reads=[k.R("yb")], sem="yb", store=True)
    k.barrier()


def phase_S(k, g, li, P, S):
    with ExitStack() as es:
        alb = sbt(k, es, "alb", [128, 16], F32)
        dskb = sbt(k, es, "dskb", [128, 16], F32)
        dsk = sbt(k, es, "dsk", [128, 8], F32)
        sng = sbt(k, es, "sng", [128, 8], F32)
        k.dma("sp", alb[:], P["a_log"][li].partition_broadcast(128), writes=[k.R("alb")], sem="alb")
        k.op("act", lambda e: e.activation(out=alb[:], in_=alb[:], func=AF.Exp), reads=[k.R("alb")], writes=[k.R("alb")])
        k.op("dve", lambda e: e.tensor_scalar(out=alb[:], in0=alb[:], scalar1=-1.0, scalar2=None, op0=ALU.mult), reads=[k.R("alb")], writes=[k.R("alb")])
        k.dma("sp", dskb[:], P["d_skip"][li].partition_broadcast(128), writes=[k.R("dskb")], sem="dskb")
        k.op("dve", lambda e: e.tensor_copy(out=dsk[0:64, :], in_=dskb[0:64, 0:16:2]), reads=[k.R("dskb")], writes=[k.R("dsk")])
        k.op("dve", lambda e: e.tensor_copy(out=dsk[64:128, :], in_=dskb[64:128, 1:16:2]), reads=[k.R("dskb"), k.R("dsk")], writes=[k.R("dsk")])
        load_cols(k, g, es, "sng", P["ssd_norm_g"][li].rearrange("(i p) -> i p", p=128), 8, sng[:])
        Sin = sbt(k, es, "Sin", [128, 2, 512], F32)
        Sbf = sbt(k, es, "Sbf", [128, 2, 512], BF16)
        k.op("dve", lambda e: e.memset(Sin[:], 0.0), writes=[k.R("Sin")])
        k.op("dve", lambda e: e.memset(Sbf[:], 0.0), writes=[k.R("Sbf")])
        xc = [sbt(k, es, "xc%d" % i, [128, 12, 128], F32) for i in range(2)]
        zc = [sbt(k, es, "zc%d" % i, [128, 8, 128], F32) for i in range(2)]
        dtc = [sbt(k, es, "dtc%d" % i, [128, 16], F32) for i in range(2)]
        da_ = [sbt(k, es, "da%d" % i, [128, 16], F32) for i in range(2)]
        acs_ = [sbt(k, es, "acs%d" % i, [128, 32], F32) for i in range(2)]
        wdec_ = [sbt(k, es, "wdec%d" % i, [128, 16], F32) for i in range(2)]
        etot_ = [sbt(k, es, "etot%d" % i, [128, 16], F32) for i in range(2)]
        xh_ = [sbt(k, es, "xh%d" % i, [128, 1024], BF16) for i in range(2)]
        xhd_ = [sbt(k, es, "xhd%d" % i, [128, 1024], BF16) for i in range(2)]
        Btm_ = [sbt(k, es, "Btm%d" % i, [128, 2, 128], BF16) for i in range(2)]
        bcb_ = [sbt(k, es, "bcb%d" % i, [128, 4, 128], BF16) for i in range(2)]
        cbm_l = [sbt(k, es, "cbm%d" % i, [128, 128], F32) for i in range(2)]
        rhsh = [sbt(k, es, "rhsh%d" % i, [128, 2, 128], F32) for i in range(4)]
        ex = [sbt(k, es, "ex%d" % i, [128, 2, 256], F32) for i in range(4)]
        MT = [sbt(k, es, "MT%d" % i, [128, 2, 128], BF16) for i in range(4)]
        Cp = [sbt(k, es, "Cp%d" % i, [128, 2, 128], BF16) for i in range(4)]
        yt__l = [sbt(k, es, "yt_%d" % i, [128, 4, 128], F32) for i in range(2)]
        ygz_ = [sbt(k, es, "ygz%d" % i, [128, 8, 128], F32) for i in range(2)]
        sqy_l = [sbt(k, es, "sqy%d" % i, [128, 512], BF16) for i in range(2)]
        sscol = sbt(k, es, "sscol", [128, 16], F32)
        ynb = [sbt(k, es, "ynb%d" % i, [128, 8, 128], BF16) for i in range(2)]
        LE = g.cf[:, CO_LE:CO_LE + 128]
        GT = g.cf[:, CO_GT:CO_GT + 128]
        ONEF = g.cf[:, CO_ONE:CO_ONE + 128]
        ONEB = g.cb[:, CO_ONE:CO_ONE + 128]
        for c in range(16):
            s = c % 2
            tsl = slice(c * 128, (c + 1) * 128)
            k.dma("sp", xc[s][:], S["xbcT"][:, tsl].rearrange("(i p) t -> p i t", p=128), writes=[k.R("xc", s)], sem="xc%d" % s)
            k.dma("sp", zc[s][:], S["zsT"][:, tsl].rearrange("(i p) t -> p i t", p=128), writes=[k.R("zc", s)], sem="zc%d" % s)
            k.dma("sp", dtc[s][:], S["dtv"][tsl, :], writes=[k.R("dtc", s)], sem="dtc%d" % s)
            X, Z, DT = xc[s], zc[s], dtc[s]
            da, acs, wdec, etot, xh, xhd, Btm, bcb, ygz = da_[s], acs_[s], wdec_[s], etot_[s], xh_[s], xhd_[s], Btm_[s], bcb_[s], ygz_[s]
            rx, rz, rdt = k.R("xc", s), k.R("zc", s), k.R("dtc", s)
            k.op("dve", lambda e: e.tensor_tensor(out=da[:], in0=DT[:], in1=alb[:], op=ALU.mult), reads=[rdt, k.R("alb")], writes=[k.R("da", s)])
            b = k.nxt("bank", 8)
            k.op("pe", lambda e: e.matmul(out=psb(g, b, 16, 0), lhsT=LE, rhs=da[:], start=True, stop=True), reads=[k.R("da", s), k.R("cf")], writes=[k.R("ps", b)], inc=False)
            k.op("pe", lambda e: e.matmul(out=psb(g, b, 16, 16), lhsT=ONEF, rhs=da[:], start=True, stop=True), reads=[k.R("da", s), k.R("cf")], writes=[k.R("ps", b)])
            k.op("dve", lambda e: e.tensor_copy(out=acs[:], in_=psb(g, b, 32)), reads=[k.R("ps", b)], writes=[k.R("acs", s)])
            k.op("dve", lambda e: e.tensor_tensor(out=wdec[:], in0=acs[:, 16:32], in1=acs[:, 0:16], op=ALU.subtract), reads=[k.R("acs", s)], writes=[k.R("wdec", s)])
            k.op("act", lambda e: e.activation(out=wdec[:], in_=wdec[:], func=AF.Exp), reads=[k.R("wdec", s)], writes=[k.R("wdec", s)])
            k.op("dve", lambda e: e.tensor_tensor(out=wdec[:], in0=wdec[:], in1=DT[:], op=ALU.mult), reads=[k.R("wdec", s), rdt], writes=[k.R("wdec", s)])
            k.op("act", lambda e: e.activation(out=etot[:], in_=acs[:, 16:32], func=AF.Exp), reads=[k.R("acs", s)], writes=[k.R("etot", s)])
            for b4 in range(2):
                b = k.nxt("bank", 8)
                for j in range(4):
                    i = b4 * 4 + j
                    k.op("pe", lambda e: e.transpose(out=psb(g, b, 128, j * 128), in_=X[:, i, :], identity=g.ident),
                         reads=[rx, k.R("cf")], writes=[k.R("ps", b)], inc=(j == 3))
                pv = psb(g, b).rearrange("p (h q) -> p h q", h=8)
                k.op("dve", lambda e: e.tensor_tensor(out=xh[:, b4 * 512:(b4 + 1) * 512].rearrange("p (h q) -> p h q", h=8), in0=pv,
                                                      in1=DT[:, b4 * 8:(b4 + 1) * 8].unsqueeze(2).to_broadcast([128, 8, 64]), op=ALU.mult),
                     reads=[k.R("ps", b), rdt], writes=[k.R("xh", s)])
                k.op("dve", lambda e: e.tensor_tensor(out=xhd[:, b4 * 512:(b4 + 1) * 512].rearrange("p (h q) -> p h q", h=8), in0=pv,
                                                      in1=wdec[:, b4 * 8:(b4 + 1) * 8].unsqueeze(2).to_broadcast([128, 8, 64]), op=ALU.mult),
                     reads=[k.R("ps", b), k.R("wdec", s)], writes=[k.R("xhd", s)])
            b = k.nxt("bank", 8)
            for j in range(2):
                k.op("pe", lambda e: e.transpose(out=psb(g, b, 128, j * 128), in_=X[:, 8 + j, :], identity=g.ident),
                     reads=[rx, k.R("cf")], writes=[k.R("ps", b)], inc=(j == 1))
            k.op("act", lambda e: e.copy(out=Btm[:], in_=psb(g, b, 256).rearrange("p (a c) -> p a c", a=2)), reads=[k.R("ps", b)], writes=[k.R("Btm", s)])
            k.op("act", lambda e: e.copy(out=bcb[:], in_=X[:, 8:12, :]), reads=[rx], writes=[k.R("bcb", s)])
            bss = k.nxt("bank", 8)
            resv = [bss]

            def freebank():
                b_ = k.nxt("bank", 8)
                while b_ in resv:
                    b_ = k.nxt("bank", 8)
                return b_
            bys = []
            for gq in range(2):
                by = freebank()
                resv.append(by)
                bys.append(by)
            for gq in range(2):
                b = freebank()
                k.op("pe", lambda e: e.matmul(out=psb(g, b, 128), lhsT=bcb[:, gq, :], rhs=bcb[:, 2 + gq, :], start=True, stop=True),
                     reads=[k.R("bcb", s)], writes=[k.R("ps", b)])
                k.op("dve", lambda e: e.tensor_tensor(out=cbm_l[gq][:], in0=psb(g, b, 128), in1=LE, op=ALU.mult), reads=[k.R("ps", b), k.R("cf")], writes=[k.R("cbm", gq)])

            def stageA(pi):
                t2 = pi % 4
                h0 = 2 * pi
                k.op("dve", lambda e: e.tensor_tensor(out=rhsh[t2][:], in0=LE.unsqueeze(1).to_broadcast([128, 2, 128]),
                                                      in1=da[:, h0:h0 + 2].unsqueeze(2).to_broadcast([128, 2, 128]), op=ALU.mult),
                     reads=[k.R("cf"), k.R("da", s)], writes=[k.R("rhsh", t2)])
                b = freebank()
                for hh in range(2):
                    k.op("pe", lambda e: e.matmul(out=psb(g, b, 128, hh * 256), lhsT=GT, rhs=rhsh[t2][:, hh, :], start=True, stop=True),
                         reads=[k.R("rhsh", t2), k.R("cf")], writes=[k.R("ps", b)], inc=False)
                    k.op("pe", lambda e: e.matmul(out=psb(g, b, 128, hh * 256 + 128), lhsT=ONEF, rhs=rhsh[t2][:, hh, :], start=True, stop=True),
                         reads=[k.R("rhsh", t2), k.R("cf")], writes=[k.R("ps", b)], inc=(hh == 1))
                k.op("act", lambda e: e.activation(out=ex[t2][:].rearrange("p a c -> p (a c)"), in_=psb(g, b, 512), func=AF.Exp),
                     reads=[k.R("ps", b)], writes=[k.R("ex", t2)])

            def stageB(pi):
                t2 = pi % 4
                gq = pi // 4
                by = bys[gq]
                k.op("dve", lambda e: e.tensor_tensor(out=MT[t2][:], in0=ex[t2][:, :, 0:128], in1=cbm_l[gq][:].unsqueeze(1).to_broadcast([128, 2, 128]), op=ALU.mult),
                     reads=[k.R("ex", t2), k.R("cbm", gq)], writes=[k.R("MT", t2)])
                k.op("dve", lambda e: e.tensor_tensor(out=Cp[t2][:], in0=ex[t2][:, :, 128:256], in1=X[:, 10 + gq, :].unsqueeze(1).to_broadcast([128, 2, 128]), op=ALU.mult),
                     reads=[k.R("ex", t2), rx], writes=[k.R("Cp", t2)])
                for hh in range(2):
                    h = 2 * pi + hh
                    r = h % 8
                    po = (r % 2) * 64
                    yout = g.ps[po:po + 64, by * 512 + (r // 2) * 128: by * 512 + (r // 2) * 128 + 128]
                    k.op("pe", lambda e: e.matmul(out=yout, lhsT=xh[:, h * 64:(h + 1) * 64], rhs=MT[t2][:, hh, :], start=True, stop=False),
                         reads=[k.R("xh", s), k.R("MT", t2)], writes=[k.R("ps", by)], inc=False)
                    k.op("pe", lambda e: e.matmul(out=yout, lhsT=Sbf[:, gq, r * 64:(r + 1) * 64], rhs=Cp[t2][:, hh, :], start=False, stop=True),
                         reads=[k.R("Sbf"), k.R("Cp", t2)], writes=[k.R("ps", by)], inc=(hh == 1))

            def epilogue(gq):
                by = bys[gq]
                yt_, sqy = yt__l[gq], sqy_l[gq]
                for pr_ in range(4):
                    i = gq * 4 + pr_
                    k.op("dve", lambda e: e.scalar_tensor_tensor(out=yt_[:, pr_, :], in0=X[:, i, :], scalar=dsk[:, i:i + 1],
                                                                 in1=psb(g, by, 128, pr_ * 128), op0=ALU.mult, op1=ALU.add),
                         reads=[rx, k.R("dsk"), k.R("ps", by)], writes=[k.R("yt_", gq)])
                k.op("dve", lambda e: e.tensor_tensor(out=ygz[:, gq * 4:(gq + 1) * 4, :], in0=yt_[:], in1=Z[:, gq * 4:(gq + 1) * 4, :], op=ALU.mult),
                     reads=[k.R("yt_", gq), rz], writes=[k.R("ygz", s)])
                k.op("act", lambda e: e.activation(out=sqy[:].rearrange("p (a c) -> p a c", a=4), in_=ygz[:, gq * 4:(gq + 1) * 4, :], func=AF.Square),
                     reads=[k.R("ygz", s)], writes=[k.R("sqy", gq)])
                for pr_ in range(4):
                    i = gq * 4 + pr_
                    k.op("pe", lambda e: e.matmul(out=psb(g, bss, 1), lhsT=sqy[:, pr_ * 128:(pr_ + 1) * 128], rhs=ONEB[:, 0:1], start=(i == 0), stop=(i == 7)),
                         reads=[k.R("sqy", gq), k.R("cb")], writes=[k.R("ps", bss)], inc=(pr_ == 3))
                b = freebank()
                k.op("pe", lambda e: e.matmul(out=psb(g, b), lhsT=Btm[:, gq, :], rhs=xhd[:, gq * 512:(gq + 1) * 512], start=True, stop=True),
                     reads=[k.R("Btm", s), k.R("xhd", s)], writes=[k.R("ps", b)])
                sv = Sin[:, gq, :].rearrange("p (h q) -> p h q", h=8)
                k.op("dve", lambda e: e.tensor_tensor(out=sv, in0=sv, in1=etot[:, gq * 8:(gq + 1) * 8].unsqueeze(2).to_broadcast([128, 8, 64]), op=ALU.mult),
                     reads=[k.R("Sin"), k.R("etot", s)], writes=[k.R("Sin")])
                k.op("dve", lambda e: e.tensor_tensor(out=Sin[:, gq, :], in0=Sin[:, gq, :], in1=psb(g, b), op=ALU.add),
                     reads=[k.R("Sin"), k.R("ps", b)], writes=[k.R("Sin")])
                k.op("act", lambda e: e.copy(out=Sbf[:, gq, :], in_=Sin[:, gq, :]), reads=[k.R("Sin")], writes=[k.R("Sbf")])

            LA = 2
            for idx in range(8 + LA):
                if idx < 8:
                    stageA(idx)
                if idx >= LA:
                    pi = idx - LA
                    stageB(pi)
                    if pi % 4 == 3:
                        epilogue(pi // 4)
            k.op("dve", lambda e: e.tensor_copy(out=sscol[:, c:c + 1], in_=psb(g, bss, 1)), reads=[k.R("ps", bss)], writes=[k.R("sscol")])
            yn = ynb[s]
            for i in range(8):
                k.op("dve", lambda e: e.tensor_scalar(out=yn[:, i, :], in0=ygz[:, i, :], scalar1=sng[:, i:i + 1], scalar2=None, op0=ALU.mult),
                     reads=[k.R("ygz", s), k.R("vec", "sng")], writes=[k.R("ynb", s)])
            for (i0, i1) in ((0, 3), (3, 6), (6, 8)):
                k.dma("sp", S["ysend%d" % (i0 // 3)][0:(i1 - i0) * 128, tsl].rearrange("(i p) t -> p i t", p=128), yn[:, i0:i1, :],
                      reads=[k.R("ynb", s)], sem="ynb%d" % s, store=True)
        b = k.nxt("bank", 8)
        sst = sbt(k, es, "sst", [16, 128], F32)
        k.op("pe", lambda e: e.transpose(out=g.ps[0:16, b * 512:b * 512 + 128], in_=sscol[:], identity=g.ident),
             reads=[k.R("sscol"), k.R("cf")], writes=[k.R("ps", b)])
        k.op("dve", lambda e: e.tensor_copy(out=sst[:], in_=g.ps[0:16, b * 512:b * 512 + 128]), reads=[k.R("ps", b)], writes=[k.R("sst")])
        k.dma("sp", S["ssp"], sst[:], reads=[k.R("sst")], sem="sst", store=True)
    k.barrier()


def phase_T(k, g, li, x_src, x_dst, p_src, P, S):
    s0 = g.poolc[:, 136:137]
    s1 = g.poolc[:, 137:138]
    with ExitStack() as es:
        mT = g.mT
        with ExitStack() as e1:
            Ysel = g.Ysel
            wB = [sbt(k, e1, "wB%d" % i, [128, 36, 512], BF16) for i in range(2)]
            gt = [sbt(k, e1, "gt%d" % i, [128, 3, 512], F32) for i in range(2)]
            m0 = sbt(k, e1, "m0", [128, 512], F32)
            m1 = sbt(k, e1, "m1", [128, 512], F32)
            ssq = sbt(k, e1, "ssq", [128, 32], F32)
            rs = sbt(k, e1, "rs", [128, 16], F32)
            load_cols(k, g, e1, "ssg", S["ssg"], 32, ssq[:])
            k.op("dve", lambda e: e.tensor_tensor(out=ssq[:, 0:16], in0=ssq[:, 0:16], in1=ssq[:, 16:32], op=ALU.add), reads=[k.R("vec", "ssg")], writes=[k.R("vec", "ssg")])
            k.op("dve", lambda e: e.tensor_scalar(out=rs[:, 0:8], in0=ssq[:, 0:8], scalar1=s0, scalar2=None, op0=ALU.mult),
                 reads=[k.R("vec", "ssg"), k.R("poolc")], writes=[k.R("rs")])
            k.op("dve", lambda e: e.scalar_tensor_tensor(out=rs[:, 0:8], in0=ssq[:, 8:16], scalar=s1, in1=rs[:, 0:8], op0=ALU.mult, op1=ALU.add),
                 reads=[k.R("vec", "ssg"), k.R("poolc"), k.R("rs")], writes=[k.R("rs")])
            k.op("act", lambda e: e.activation(out=rs[:, 8:16], in_=rs[:, 0:8], func=AF.Sqrt, scale=1.0 / 2048, bias=g.epsb[:, 0:1]),
                 reads=[k.R("rs"), k.R("epsb")], writes=[k.R("rs")])
            k.op("dve", lambda e: e.reciprocal(out=rs[:, 0:8], in_=rs[:, 8:16]), reads=[k.R("rs")], writes=[k.R("rs")])
            ysel_build(k, g, S, range(15, 18))
            Yall = [k.R("Ysel", kt) for kt in range(36)]
            branch = [(0 if i < 8 else (1 if i < 16 else 2)) for i in range(18)] * 2
            m0s = [m0, sbt(k, e1, "m0b", [128, 512], F32)]
            m1s = [m1, m1]
            pend = []

            def finish(item):
                par, dg_, tt_ = item
                mm = m0s[par]
                b = k.nxt("bank", 8)
                for j in range(4):
                    k.op("pe", lambda e: e.transpose(out=psb(g, b, 128, j * 128), in_=mm[:, j * 128:(j + 1) * 128], identity=g.ident),
                         reads=[k.R("m0", par), k.R("cf")], writes=[k.R("ps", b)], inc=(j == 3))
                k.op("act", lambda e: e.copy(out=mT[:, dg_ * 4:(dg_ + 1) * 4, tt_ * 128:(tt_ + 1) * 128], in_=psb(g, b).rearrange("p (a c) -> p a c", a=4)),
                     reads=[k.R("ps", b)], writes=[k.R("mT", tt_)])
            for dg in range(4):
                ws = dg % 2
                csl = slice(dg * 512, (dg + 1) * 512)
                k.dma("pool", wB[ws][:, 0:18, :], P["w_br"][li][0:2304, csl].rearrange("(kt p) c -> p kt c", p=128), writes=[k.R("wB", ws)], sem="wB%d" % ws)
                k.dma("pool", wB[ws][:, 18:36, :], P["w_br"][li][2304:4608, csl].rearrange("(kt p) c -> p kt c", p=128), writes=[k.R("wB", ws)], sem="wB%d" % ws)
                for tt in range(NT):
                    s = k.nxt("gt", 2)
                    par = k.nxt("m0par", 2)
                    mm0, mm1 = m0s[par], m1s[par]
                    tsl = slice(tt * 128, (tt + 1) * 128)
                    k.dma("sp", gt[s][:], S["gates"][tsl, :].rearrange("p (a c) -> p a c", a=3)[:, :, csl], writes=[k.R("gt", s)], sem="gt%d" % s)
                    bs = [k.nxt("bank", 8) for _ in range(3)]
                    for br in range(3):
                        idx = [i for i in range(36) if branch[i] == br]
                        for n_, i in enumerate(idx):
                            k.op("pe", lambda e: e.matmul(out=psb(g, bs[br]), lhsT=Ysel[:, i, tsl], rhs=wB[ws][:, i, :], start=(n_ == 0), stop=(n_ == len(idx) - 1)),
                                 reads=[Yall[i], k.R("wB", ws)], writes=[k.R("ps", bs[br])], inc=(n_ == len(idx) - 1))
                    if pend:
                        finish(pend.pop(0))
                    k.op("dve", lambda e: e.scalar_tensor_tensor(out=mm0[:], in0=psb(g, bs[0]), scalar=rs[:, tt:tt + 1], in1=gt[s][:, 0, :], op0=ALU.mult, op1=ALU.mult),
                         reads=[k.R("ps", bs[0]), k.R("gt", s), k.R("rs")], writes=[k.R("m0", par)])
                    k.op("dve", lambda e: e.tensor_tensor(out=mm1[:], in0=psb(g, bs[1]), in1=gt[s][:, 1, :], op=ALU.mult), reads=[k.R("ps", bs[1]), k.R("gt", s)], writes=[k.R("m1")])
                    k.op("dve", lambda e: e.tensor_tensor(out=mm0[:], in0=mm0[:], in1=mm1[:], op=ALU.add), reads=[k.R("m0", par), k.R("m1")], writes=[k.R("m0", par)])
                    k.op("dve", lambda e: e.tensor_tensor(out=mm1[:], in0=psb(g, bs[2]), in1=gt[s][:, 2, :], op=ALU.mult), reads=[k.R("ps", bs[2]), k.R("gt", s), k.R("m1")], writes=[k.R("m1")])
                    k.op("dve", lambda e: e.tensor_tensor(out=mm0[:], in0=mm0[:], in1=mm1[:], op=ALU.add), reads=[k.R("m0", par), k.R("m1")], writes=[k.R("m0", par)])
                    pend.append((par, dg, tt))
            while pend:
                finish(pend.pop(0))
            k.barrier()
        g.el.close()
        with ExitStack() as e2:
            wO = sbt(k, e2, "wO", [128, 16, D], BF16)
            xt = [sbt(k, e2, "txt%d" % i, [128, D], F32) for i in range(2)]
            xn = sbt(k, e2, "txn", [128, D], F32)
            gB2 = sbt(k, e2, "gB2", [128, D], F32)
            junk = sbt(k, e2, "tjunk", [128, D], BF16)
            stat = sbt(k, e2, "tstat", [128, 48], F32)
            for cg in range(4):
                k.dma("pool", wO[:, :, cg * 512:(cg + 1) * 512], P["w_out"][li][:, cg * 512:(cg + 1) * 512].rearrange("(kt p) c -> p kt c", p=128),
                      writes=[k.R("wO", cg)], sem="wO%d" % cg)
            k.dma("sp", gB2[:], P["ple_norm_g"][li].partition_broadcast(128), writes=[k.R("gB2")], sem="gB2")
            xns = [xn, sbt(k, e2, "txnb", [128, D], F32)]
            pend2 = []

            def finish2(item):
                tt_, s_, half_ = item
                for b4 in range(4):
                    b = half_ * 4 + b4
                    for j in range(4):
                        kt = b4 * 4 + j
                        k.op("pe", lambda e: e.transpose(out=psb(g, b, 128, j * 128), in_=xns[s_][:, kt * 128:(kt + 1) * 128], identity=g.ident),
                             reads=[k.R("txn", s_), k.R("cf")], writes=[k.R("ps", b)], inc=(j == 3))
                    k.op("act", lambda e: e.copy(out=mT[:, b4 * 4:(b4 + 1) * 4, tt_ * 128:(tt_ + 1) * 128], in_=psb(g, b).rearrange("p (a c) -> p a c", a=4)),
                         reads=[k.R("ps", b)], writes=[k.R("mT", tt_)])
            for tt in range(NT):
                s = tt % 2
                tsl = slice(tt * 128, (tt + 1) * 128)
                k.dma("sp", xt[s][:], x_src[tsl, :], writes=[k.R("txt", s)], sem="txt%d" % s)
                half = k.nxt("half", 2)
                for eg in range(4):
                    b = half * 4 + eg
                    for kt in range(16):
                        k.op("pe", lambda e: e.matmul(out=psb(g, b), lhsT=mT[:, kt, tsl], rhs=wO[:, kt, eg * 512:(eg + 1) * 512], start=(kt == 0), stop=(kt == 15)),
                             reads=[k.R("mT", tt), k.R("wO", eg)], writes=[k.R("ps", b)], inc=(kt == 15))
                if pend2:
                    finish2(pend2.pop(0))
                pa = g.ps[:, half * 2048:(half + 1) * 2048]
                pr = [k.R("ps", half * 4 + t) for t in range(4)]
                k.op("dve", lambda e: e.tensor_tensor(out=xt[s][:], in0=pa, in1=xt[s][:], op=ALU.add), reads=pr + [k.R("txt", s)], writes=[k.R("txt", s)])
                k.dma("sp", S["xmid"][tsl, :], xt[s][:], reads=[k.R("txt", s)], sem="txo%d" % s, store=True)
                k.op("act", lambda e: e.activation(out=junk[:], in_=xt[s][:], func=AF.Square, accum_out=stat[:, tt:tt + 1]),
                     reads=[k.R("txt", s)], writes=[k.R("tjunk"), k.R("tst", tt)])
                k.op("act", lambda e: e.activation(out=stat[:, 16 + tt:17 + tt], in_=stat[:, tt:tt + 1], func=AF.Sqrt, scale=1.0 / D, bias=g.epsb[:, 0:1]),
                     reads=[k.R("tst", tt), k.R("epsb")], writes=[k.R("tst2", tt)])
                k.op("dve", lambda e: e.reciprocal(out=stat[:, 32 + tt:33 + tt], in_=stat[:, 16 + tt:17 + tt]), reads=[k.R("tst2", tt)], writes=[k.R("tst3", tt)])
                k.op("dve", lambda e: e.scalar_tensor_tensor(out=xns[s][:], in0=xt[s][:], scalar=stat[:, 32 + tt:33 + tt], in1=gB2[:], op0=ALU.mult, op1=ALU.mult),
                     reads=[k.R("txt", s), k.R("tst3", tt), k.R("gB2")], writes=[k.R("txn", s)])
                pend2.append((tt, s, half))
            while pend2:
                finish2(pend2.pop(0))
            k.barrier()
        k.fence()
        with ExitStack() as e3:
            wG = [sbt(k, e3, "wG%d" % i, [128, 16, 512], BF16) for i in range(2)]
            wPp = [sbt(k, e3, "wPp%d" % i, [128, 2, 512], BF16) for i in range(2)]
            pT = sbt(k, e3, "pT", [128, 2, LT], BF16)
            pt = [sbt(k, e3, "pt%d" % i, [128, 256], F32) for i in range(2)]
            sg = [sbt(k, e3, "sg%d" % i, [128, 512], F32) for i in range(2)]
            xm2 = [sbt(k, e3, "xm2%d" % i, [128, 512], F32) for i in range(2)]
            for tt in range(NT):
                s = tt % 2
                tsl = slice(tt * 128, (tt + 1) * 128)
                k.dma("sp", pt[s][:], p_src[tsl, :], writes=[k.R("pt", s)], sem="pt%d" % s)
                b = k.nxt("bank", 8)
                for j in range(2):
                    k.op("pe", lambda e: e.transpose(out=psb(g, b, 128, j * 128), in_=pt[s][:, j * 128:(j + 1) * 128], identity=g.ident),
                         reads=[k.R("pt", s), k.R("cf")], writes=[k.R("ps", b)], inc=(j == 1))
                k.op("act", lambda e: e.copy(out=pT[:, :, tsl], in_=psb(g, b, 256).rearrange("p (a c) -> p a c", a=2)), reads=[k.R("ps", b)], writes=[k.R("pT")])
            for eg in range(4):
                ws = eg % 2
                csl = slice(eg * 512, (eg + 1) * 512)
                k.dma("pool", wG[ws][:], P["w_ple_gate"][li][:, csl].rearrange("(kt p) c -> p kt c", p=128), writes=[k.R("wG", ws)], sem="wG%d" % ws)
                k.dma("pool", wPp[ws][:], P["w_ple_proj"][li][:, csl].rearrange("(kt p) c -> p kt c", p=128), writes=[k.R("wPp", ws)], sem="wPp%d" % ws)
                for tt in range(NT):
                    tsl = slice(tt * 128, (tt + 1) * 128)
                    s = k.nxt("sg", 2)
                    k.dma("sp", xm2[s][:], S["xmid"][tsl, csl], writes=[k.R("xm2", s)], sem="xm2%d" % s)
                    b1 = k.nxt("bank", 8)
                    for kt in range(16):
                        k.op("pe", lambda e: e.matmul(out=psb(g, b1), lhsT=mT[:, kt, tsl], rhs=wG[ws][:, kt, :], start=(kt == 0), stop=(kt == 15)),
                             reads=[k.R("mT", tt), k.R("wG", ws)], writes=[k.R("ps", b1)], inc=(kt == 15))
                    b2 = k.nxt("bank", 8)
                    for kt in range(2):
                        k.op("pe", lambda e: e.matmul(out=psb(g, b2), lhsT=pT[:, kt, tsl], rhs=wPp[ws][:, kt, :], start=(kt == 0), stop=(kt == 1)),
                             reads=[k.R("pT"), k.R("wPp", ws)], writes=[k.R("ps", b2)], inc=(kt == 1))
                    k.op("act", lambda e: e.activation(out=sg[s][:], in_=psb(g, b1), func=AF.Sigmoid), reads=[k.R("ps", b1)], writes=[k.R("sg", s)])
                    k.op("dve", lambda e: e.tensor_tensor(out=sg[s][:], in0=sg[s][:], in1=psb(g, b2), op=ALU.mult), reads=[k.R("sg", s), k.R("ps", b2)], writes=[k.R("sg", s)])
                    k.op("dve", lambda e: e.tensor_tensor(out=sg[s][:], in0=sg[s][:], in1=xm2[s][:], op=ALU.add), reads=[k.R("sg", s), k.R("xm2", s)], writes=[k.R("sg", s)])
                    k.dma("sp", x_dst[tsl, csl], sg[s][:], reads=[k.R("sg", s)], sem="sg%d" % s, store=True)
    k.barrier()


def ysel_build(k, g, S, tiles, fence=True):
    s0 = g.poolc[:, 136:137]
    s1 = g.poolc[:, 137:138]
    if fence:
        k.fence(("sp",))
    else:
        d = k.dsems["cc_y"]
        k._wait("sp", [(d.sem, d.count - 1)])
    for q in range(2):
        for i in tiles:
            kt = q * 18 + i
            s = k.nxt("Gl", 2)
            k.dma("sp", g.Gl[s][:], S["yg%d" % (i // 3)][q * 384 + (i % 3) * 128:q * 384 + (i % 3 + 1) * 128, :], writes=[k.R("Gl", s)], sem="Gl%d" % s)
            k.op("dve", lambda e: e.tensor_scalar(out=g.Ysel[:, kt, :], in0=g.Gl[s][:, 0:LT], scalar1=s0, scalar2=None, op0=ALU.mult),
                 reads=[k.R("Gl", s), k.R("poolc")], writes=[k.R("Ysel", kt)])
            k.op("dve", lambda e: e.scalar_tensor_tensor(out=g.Ysel[:, kt, :], in0=g.Gl[s][:, LT:L], scalar=s1, in1=g.Ysel[:, kt, :], op0=ALU.mult, op1=ALU.add),
                 reads=[k.R("Gl", s), k.R("poolc"), k.R("Ysel", kt)], writes=[k.R("Ysel", kt)])


def build_program(nc, depth=DEPTH, kind="Internal", phases=("A", "P", "AT", "S", "X", "T"), ncores=8):
    x_ap = nc.dram_tensor("x", [LT, D], F32, kind="ExternalInput").ap()
    p_ap = nc.dram_tensor("p", [depth, LT, 256], F32, kind="ExternalInput").ap()
    c_ap = nc.dram_tensor("consts", [128, 640], F32, kind="ExternalInput").ap()
    sel_ap = nc.dram_tensor("sel", [128, 144], F32, kind="ExternalInput").ap()
    out_ap = nc.dram_tensor("out", [LT, D], F32, kind="ExternalOutput").ap()
    P = {n: nc.dram_tensor(n, [depth] + shp[1:], F32, kind="ExternalInput").ap() for n, shp in PARAM_SHAPES.items()}
    S = alloc_scratch(nc, kind=kind)
    if kind != "Internal":
        for n in ["hTb0", "hTb1", "hTg0", "hTg1", "ssp", "ssg"] + ["ysend%d" % c for c in range(6)] + ["yg%d" % c for c in range(6)]:
            shp, dt = list(S[n].shape), S[n].dtype
            S[n] = nc.dram_tensor(n + "_i", shp, dt, kind="Internal").ap()
    with ExitStack() as es:
        k = K(nc, es)
        g = Ctx()
        g.pairs = [[2 * i, 2 * i + 1] for i in range(ncores // 2)]
        setup_globals(k, es, g, c_ap, sel_ap)
        for li in range(depth):
            k.sfx = "_L%d" % li
            x_src = x_ap if li == 0 else S["x1"]
            x_dst = out_ap if li == depth - 1 else S["x1"]
            if "A" in phases:
                phase_A(k, g, li, x_src, P, S)
            k.fence()
            em = ExitStack()
            g.mT = sbt(k, em, "mT", [128, 16, LT], BF16)
            el = ExitStack()
            g.el = el
            g.Ysel = sbt(k, el, "Ysel", [128, 36, LT], BF16)
            g.Gl = [sbt(k, el, "Gl%d" % i, [128, L], BF16) for i in range(2)]
            phase_S(k, g, li, P, S)
            for c in range(2):
                k.collective("AllGather", S["ysend%d" % c], S["yg%d" % c], g.pairs, "y")
            k.collective("AllGather", S["ssp"], S["ssg"], g.pairs, "s")
            phase_P(k, g, li, P, S)
            for c in range(2, 5):
                k.collective("AllGather", S["ysend%d" % c], S["yg%d" % c], g.pairs, "y")
            phase_AT(k, g, li, P, S)
            k.collective("AllGather", S["ysend5"], S["yg5"], g.pairs, "y")
            ysel_build(k, g, S, range(0, 15), fence=False)
            k.fence()
            if "T" in phases:
                phase_T(k, g, li, x_src, x_dst, p_ap[li], P, S)
            em.close()
            k.fence()
        k.barrier()
    return k


def _sel_table(r):
    t = np.zeros((128, 144), np.float32)
    for gl in range(2):
        w = (2, 4, 8, 16)[2 * r + gl]
        j = int(np.log2(w)) - 1
        t[:, gl * 4 + j] = 1.0 / w
        t[:, 8 + (gl * 4 + j) * 16: 8 + (gl * 4 + j) * 16 + 16] = 1.0 / np.minimum(np.arange(16) + 1, w)
    t[:, 136 + r] = 1.0
    return t


def _core_params(inputs, r, depth=DEPTH):
    f = lambda a: np.ascontiguousarray(a[:depth], dtype=np.float32)
    w_in = inputs["w_in"]
    o = np.cumsum([0, 2048, 3072, 32, 2048, 2048, 1536, 1536, 1536, 512])
    ZS, XBC, DT_, UP, ZP, Q_, K_, V_, ZA, G_ = [int(v) for v in o]
    heads = [gi * 4 + 2 * r + jl for gi in range(3) for jl in range(2)]
    cols = []
    cols += list(range(ZS + r * 1024, ZS + (r + 1) * 1024))
    cols += list(range(XBC + r * 1024, XBC + (r + 1) * 1024))
    cols += list(range(XBC + 2048 + r * 256, XBC + 2048 + (r + 1) * 256))
    cols += list(range(XBC + 2560 + r * 256, XBC + 2560 + (r + 1) * 256))
    cols += list(range(UP + r * 1024, UP + (r + 1) * 1024))
    cols += list(range(ZP + r * 1024, ZP + (r + 1) * 1024))
    for base in (Q_, K_):
        for h in heads:
            cols += list(range(base + h * 128, base + (h + 1) * 128))
    cols += list(range(ZA + 2 * r * 128, ZA + (2 * r + 2) * 128))
    cols += list(range(DT_ + r * 16, DT_ + (r + 1) * 16))
    for gi in range(3):
        cols += list(range(V_ + (gi * 4 + 2 * r) * 128, V_ + (gi * 4 + 2 * r + 2) * 128))
    cols += list(range(G_, G_ + 6144))
    cols = np.asarray(cols)
    assert cols.size == NCOL
    xbc_idx = np.concatenate([np.arange(r * 1024, (r + 1) * 1024), 2048 + np.arange(r * 256, (r + 1) * 256), 2560 + np.arange(r * 256, (r + 1) * 256)])
    m = {}
    m["norm_g"] = f(inputs["norm_g"])
    m["w_in"] = np.ascontiguousarray(w_in[:depth][:, :, cols], dtype=np.float32)
    m["conv_w"] = np.ascontiguousarray(inputs["conv_w"][:depth][:, :, xbc_idx], dtype=np.float32)
    m["conv_b"] = np.ascontiguousarray(inputs["conv_b"][:depth][:, xbc_idx], dtype=np.float32)
    for n in ("dt_bias", "a_log", "d_skip"):
        m[n] = np.ascontiguousarray(inputs[n][:depth][:, r * 16:(r + 1) * 16], dtype=np.float32)
    m["ssd_norm_g"] = np.ascontiguousarray(inputs["ssd_norm_g"][:depth][:, r * 1024:(r + 1) * 1024], dtype=np.float32)
    rows = []
    for q in range(2):
        rows.append(inputs["w_br_ssd"][:depth][:, q * 1024:(q + 1) * 1024])
        rows.append(inputs["w_br_pool"][:depth][:, q * 1024:(q + 1) * 1024])
        rows.append(inputs["w_br_attn"][:depth][:, q * 256:(q + 1) * 256])
    m["w_br"] = np.ascontiguousarray(np.concatenate(rows, axis=1), dtype=np.float32)
    m["pool_w"] = np.ascontiguousarray(inputs["pool_w"][:depth][:, 2 * r:2 * r + 2], dtype=np.float32)
    m["pool_scale"] = np.ascontiguousarray(inputs["pool_scale"][:depth][:, r * 1024:(r + 1) * 1024], dtype=np.float32)
    m["qk_g"] = np.ascontiguousarray(np.stack([inputs["q_norm_g"][:depth], inputs["k_norm_g"][:depth]], axis=1), dtype=np.float32)
    for n in ("w_out", "ple_norm_g", "w_ple_gate", "w_ple_proj"):
        m[n] = f(inputs[n])
    m["consts"] = _consts()
    m["sel"] = _sel_table(r)
    return m


def make_inmaps(inputs, depth=DEPTH, nb=4):
    par = [_core_params(inputs, r, depth) for r in range(2)]
    maps = []
    for b in range(nb):
        for r in range(2):
            m = dict(par[r])
            m["x"] = np.ascontiguousarray(inputs["x"][b, r * LT:(r + 1) * LT], dtype=np.float32)
            m["p"] = np.ascontiguousarray(inputs["p"][:depth, b, r * LT:(r + 1) * LT], dtype=np.float32)
            maps.append(m)
    return maps


def kernel(**inputs):
    nc = bass.Bass("TRN2", target_bir_lowering=False)
    build_program(nc)
    maps = make_inmaps(inputs)
    res = run_bass_kernel_spmd(nc, maps, core_ids=list(range(8)))
    out = np.zeros((4, L, D), np.float32)
    for b in range(4):
        for r in range(2):
            out[b, r * LT:(r + 1) * LT] = np.asarray(res.results[2 * b + r]["out"], dtype=np.float32)
    return out
```

```python
import numpy as np
from contextlib import ExitStack
import concourse.bass as bass
import concourse.mybir as mybir
from concourse.bass_utils import run_bass_kernel_spmd

F32 = mybir.dt.float32
BF16 = mybir.dt.bfloat16
AF = mybir.ActivationFunctionType
ALU = mybir.AluOpType
AX = mybir.AxisListType

L = 2048
D = 2048
NIN = 20512
DEPTH = 2
EPS = 1e-6
DIL = (1, 4, 16)
LT = 1024
NT = 8
K_ZS, K_XBC, K_UP, K_ZP, K_Q, K_K, K_ZA, K_DT, K_V, K_G = 0, 1024, 2560, 3584, 4608, 5376, 6144, 6400, 6416, 7184
NCOL = 13328
PAIRS = [[0, 1], [2, 3], [4, 5], [6, 7]]


class Reg:
    __slots__ = ("name", "w", "rs")

    def __init__(self, name):
        self.name = name
        self.w = None
        self.rs = []


class DSem:
    __slots__ = ("sem", "count", "store")

    def __init__(self, sem, store):
        self.sem = sem
        self.count = 0
        self.store = store


class K:
    def __init__(self, nc, es):
        self.nc = nc
        self.es = es
        self.eng = {"pe": nc.tensor, "act": nc.scalar, "dve": nc.vector, "pool": nc.gpsimd, "sp": nc.sync}
        self.esem = {e: es.enter_context(nc.semaphore("s_" + e)) for e in self.eng}
        self.ecnt = {e: 0 for e in self.eng}
        self.waited = {e: {} for e in self.eng}
        self.regs = {}
        self.dsems = {}
        self.dsem_by_id = {}
        self.semobj = {}
        for e, s in self.esem.items():
            self.semobj[id(s)] = s
        self.rr = {}

    def R(self, *key):
        r = self.regs.get(key)
        if r is None:
            r = Reg(key)
            self.regs[key] = r
        return r

    def nxt(self, name, n):
        v = self.rr.get(name, 0)
        self.rr[name] = v + 1
        return v % n

    def dsem(self, name, store=False):
        d = self.dsems.get(name)
        if d is None:
            sem = self.es.enter_context(self.nc.semaphore("d_" + name))
            d = DSem(sem, store)
            self.dsems[name] = d
            self.dsem_by_id[id(sem)] = d
            self.semobj[id(sem)] = sem
        return d

    def _wait(self, e, toks):
        need = {}
        for (sem, val) in toks:
            d = self.dsem_by_id.get(id(sem))
            if d is not None and not d.store:
                val = d.count
            if need.get(id(sem), 0) < val:
                need[id(sem)] = val
        for sid, val in need.items():
            if self.waited[e].get(sid, 0) < val:
                self.eng[e].wait_ge(self.semobj[sid], val)
                self.waited[e][sid] = val

    def _deps(self, e, reads, writes):
        own = id(self.esem[e]) if e in self.esem else None
        toks = []
        for r in reads:
            if r.w is not None:
                if not (e == "pe" and id(r.w[0]) == own):
                    toks.append(r.w)
        for w in writes:
            if w.w is not None and not (e == "pe" and id(w.w[0]) == own):
                toks.append(w.w)
            for t in w.rs:
                if not (e == "pe" and id(t[0]) == own):
                    toks.append(t)
        return toks

    def op(self, e, fn, reads=(), writes=(), inc=True):
        writes = list(writes) + [r for r in reads if r.name[0] == "ps"]
        self._wait(e, self._deps(e, reads, writes))
        ins = fn(self.eng[e])
        if inc:
            self.ecnt[e] += 1
            ins.then_inc(self.esem[e], 1)
            tok = (self.esem[e], self.ecnt[e])
        else:
            tok = (self.esem[e], self.ecnt[e] + 1)
        for w in writes:
            w.w = tok
            w.rs = []
        for r in reads:
            r.rs.append(tok)
            if len(r.rs) > 64:
                r.rs = r.rs[-48:]
        return ins

    def dma(self, q, out, in_, reads=(), writes=(), sem=None, store=False, **kw):
        d = self.dsem(sem, store)
        self._wait(q, self._deps(q, reads, writes))
        d.count += 16
        self.eng[q].dma_start(out=out, in_=in_, **kw).then_inc(d.sem, 16)
        tok = (d.sem, d.count)
        for w in writes:
            w.w = tok
            w.rs = []
        for r in reads:
            r.rs.append(tok)

    def collective(self, kind, in_ap, out_ap, groups, name):
        d = self.dsem("cc_" + name, True)
        self.fence(("pool",))
        d.count += 1
        self.eng["pool"].collective_compute(kind, ALU.bypass, replica_groups=groups, ins=[in_ap], outs=[out_ap]).then_inc(d.sem, 1)

    def fence(self, engines=("sp", "pool")):
        for e in engines:
            toks = [(d.sem, d.count) for d in self.dsems.values() if d.store and d.count > 0]
            self._wait(e, toks)

    def barrier(self):
        toks = [(self.esem[e], self.ecnt[e]) for e in self.eng if self.ecnt[e] > 0]
        toks += [(d.sem, d.count) for d in self.dsems.values() if d.count > 0]
        for e in self.eng:
            self._wait(e, toks)
        for r in self.regs.values():
            r.w = None
            r.rs = []


def _consts():
    c = np.zeros((128, 640), np.float32)
    i = np.arange(128)
    c[:, 0:128] = np.eye(128, dtype=np.float32)
    c[:, 128:256] = (i[:, None] <= i[None, :])
    c[:, 256:384] = (i[:, None] >= i[None, :])
    c[:, 384:512] = (i[:, None] > i[None, :])
    c[:, 512:640] = 1.0
    return c


CO_ID, CO_LE, CO_GE, CO_GT, CO_ONE, CO_INVC = 0, 128, 256, 384, 512, 640


class Ctx:
    pass


def sbt(k, es, name, shape, dt):
    return es.enter_context(k.nc.sbuf_tensor(name + getattr(k, "sfx", ""), shape, dt))


def load_consts(k, es, g, consts_ap):
    nc = k.nc
    g.cf = sbt(k, es, "cf", [128, 640], F32)
    g.cb = sbt(k, es, "cb", [128, 640], BF16)
    k.dma("sp", g.cf[:], consts_ap, writes=[k.R("cf")], sem="cf")
    k.op("dve", lambda e: e.tensor_copy(out=g.cb[:], in_=g.cf[:]), reads=[k.R("cf")], writes=[k.R("cb")])
    g.ps = es.enter_context(nc.psum_tensor("ps", [128, 4096], F32))
    g.ident = g.cf[:, CO_ID:CO_ID + 128]


def psb(g, b, n=512, off=0):
    return g.ps[:, b * 512 + off: b * 512 + off + n]


def load_cols(k, g, es, name, rows_ap, nrows, dst):
    tmp = sbt(k, es, "lc_" + name, [128, 128], F32)
    k.op("dve", lambda e: e.memset(tmp[:], 0.0), writes=[k.R("lc", name)])
    k.dma("sp", tmp[0:nrows, :], rows_ap, writes=[k.R("lc", name)], sem="lc_" + name)
    b = k.nxt("bank", 8)
    k.op("pe", lambda e: e.transpose(out=psb(g, b, 128), in_=tmp[:], identity=g.ident),
         reads=[k.R("lc", name), k.R("cf")], writes=[k.R("ps", b)])
    k.op("dve", lambda e: e.tensor_copy(out=dst, in_=psb(g, b, nrows)), reads=[k.R("ps", b)], writes=[k.R("vec", name)])


def wload(k, g, Wl, c0, cw):
    s = k.nxt("wslot", 2)
    k.dma("pool", g.wA[s][:, :, 0:cw], Wl[:, c0:c0 + cw].rearrange("(kt p) c -> p kt c", p=128),
          writes=[k.R("wA", s)], sem="wA%d" % s)
    return s


def phase_A(k, g, li, x_src, P, S):
    nc = k.nc
    with ExitStack() as es:
        g.wA = [sbt(k, es, "wA%d" % i, [128, 16, 512], BF16) for i in range(2)]
        Wl = P["w_in"][li]
        es0 = ExitStack()
        hTo = sbt(k, es0, "hTo", [128, 16, LT], BF16)
        gB = sbt(k, es0, "gB", [128, D], F32)
        xt = [sbt(k, es0, "xt%d" % i, [128, D], F32) for i in range(2)]
        xn = [sbt(k, es0, "xn%d" % i, [128, D], F32) for i in range(2)]
        junk = sbt(k, es0, "junk", [128, D], BF16)
        stat = sbt(k, es0, "stat", [128, 48], F32)
        stq = [sbt(k, es0, "stq%d" % i, [128, 512], F32) for i in range(2)]
        k.dma("sp", gB[:], P["norm_g"][li].partition_broadcast(128), writes=[k.R("gB")], sem="gB")
        for tt in range(NT):
            s = tt % 2
            k.dma("sp", xt[s][:], x_src[tt * 128:(tt + 1) * 128, :], writes=[k.R("xt", s)], sem="xt%d" % s)
            k.op("act", lambda e: e.activation(out=junk[:], in_=xt[s][:], func=AF.Square, accum_out=stat[:, tt:tt + 1]),
                 reads=[k.R("xt", s)], writes=[k.R("junk"), k.R("st", tt)])
            k.op("act", lambda e: e.activation(out=stat[:, 16 + tt:17 + tt], in_=stat[:, tt:tt + 1], func=AF.Sqrt,
                                               scale=1.0 / D, bias=g.epsb[:, 0:1]),
                 reads=[k.R("st", tt), k.R("epsb")], writes=[k.R("st2", tt)])
            k.op("dve", lambda e: e.reciprocal(out=stat[:, 32 + tt:33 + tt], in_=stat[:, 16 + tt:17 + tt]),
                 reads=[k.R("st2", tt)], writes=[k.R("st3", tt)])
            k.op("dve", lambda e: e.scalar_tensor_tensor(out=xn[s][:], in0=xt[s][:], scalar=stat[:, 32 + tt:33 + tt],
                                                         in1=gB[:], op0=ALU.mult, op1=ALU.mult),
                 reads=[k.R("xt", s), k.R("st3", tt), k.R("gB")], writes=[k.R("xn", s)])
            for b4 in range(4):
                b = k.nxt("bank", 8)
                for j in range(4):
                    kt = b4 * 4 + j
                    k.op("pe", lambda e: e.transpose(out=psb(g, b, 128, j * 128), in_=xn[s][:, kt * 128:(kt + 1) * 128],
                                                     identity=g.ident),
                         reads=[k.R("xn", s), k.R("cf")], writes=[k.R("ps", b)], inc=(j == 3))
                dst = hTo[:, b4 * 4:(b4 + 1) * 4, tt * 128:(tt + 1) * 128]
                src = psb(g, b).rearrange("p (a c) -> p a c", a=4)
                if b4 % 2 == 0:
                    k.op("act", lambda e: e.copy(out=dst, in_=src), reads=[k.R("ps", b)], writes=[k.R("hTo", tt)])
                else:
                    k.op("dve", lambda e: e.tensor_copy(out=dst, in_=src), reads=[k.R("ps", b)], writes=[k.R("hTo", tt)])
        hTo_all = [k.R("hTo", tt) for tt in range(NT)]
        for c in range(2):
            k.dma("sp", S["hTb%d" % c].rearrange("(kt p) t -> p kt t", p=128), hTo[:, c * 8:(c + 1) * 8, :], reads=hTo_all, sem="hTb", store=True)
        for c in range(2):
            k.collective("AllGather", S["hTb%d" % c], S["hTg%d" % c], g.pairs, "h")
        for cg in range(12):
            s = wload(k, g, Wl, K_G + cg * 512, 512)
            for tt in range(NT):
                b = k.nxt("bank", 8)
                for kt in range(16):
                    k.op("pe", lambda e: e.matmul(out=psb(g, b), lhsT=hTo[:, kt, tt * 128:(tt + 1) * 128], rhs=g.wA[s][:, kt, :],
                                                  start=(kt == 0), stop=(kt == 15)),
                         reads=[k.R("wA", s), k.R("hTo", tt)], writes=[k.R("ps", b)], inc=(kt == 15))
                s2 = k.nxt("stq", 2)
                k.op("act", lambda e: e.activation(out=stq[s2][:], in_=psb(g, b), func=AF.Sigmoid), reads=[k.R("ps", b)], writes=[k.R("stq", s2)])
                k.dma("sp", S["gates"][tt * 128:(tt + 1) * 128, cg * 512:(cg + 1) * 512], stq[s2][:], reads=[k.R("stq", s2)],
                      sem="stq%d" % s2, store=True)
        k.barrier()
        es0.close()
        hT = sbt(k, es, "hT", [128, 16, L], BF16)
        k.fence(("sp",))
        for q in range(2):
            for c in range(2):
                k.dma("sp", hT[:, c * 8:(c + 1) * 8, q * LT:(q + 1) * LT],
                      S["hTg%d" % c][q * 1024:(q + 1) * 1024, :].rearrange("(kt p) t -> p kt t", p=128), writes=[k.R("hT")], sem="hTl")
        hT_all = [k.R("hT")]
        Wk = [sbt(k, es, "Wk%d" % i, [128, L + 16], F32) for i in range(2)]

        pend_cm = []

        def proj_cm(c0, ntiles, epi):
            t = 0
            while t < ntiles:
                nt = min(4, ntiles - t)
                s = wload(k, g, Wl, c0 + t * 128, nt * 128)
                for ct in range(nt):
                    half = k.nxt("half", 2)
                    for tg in range(4):
                        b = half * 4 + tg
                        for kt in range(16):
                            k.op("pe", lambda e: e.matmul(out=psb(g, b), lhsT=g.wA[s][:, kt, ct * 128:(ct + 1) * 128],
                                                          rhs=hT[:, kt, tg * 512:(tg + 1) * 512], start=(kt == 0), stop=(kt == 15)),
                                 reads=[k.R("wA", s)] + hT_all, writes=[k.R("ps", b)], inc=(kt == 15))
                    if pend_cm:
                        pend_cm.pop(0)()
                    dfr = epi(t + ct, g.ps[:, half * 2048:(half + 1) * 2048], [k.R("ps", half * 4 + i) for i in range(4)], half)
                    if dfr is not None:
                        pend_cm.append(dfr)
                t += nt
            while pend_cm:
                pend_cm.pop(0)()

        stg = [sbt(k, es, "stg%d" % i, [128, L], F32) for i in range(2)]

        def epi_silu(dst, func=AF.Silu):
            def f(i, pa, pr, half):
                s = k.nxt("stg", 2)
                k.op("act", lambda e: e.activation(out=stg[s][:], in_=pa, func=func), reads=pr, writes=[k.R("stg", s)])
                k.dma("sp", dst[i * 128:(i + 1) * 128, :], stg[s][:], reads=[k.R("stg", s)], sem="stg%d" % s, store=True)
            return f

        proj_cm(K_ZS, 8, epi_silu(S["zsT"]))
        proj_cm(K_ZP, 8, epi_silu(S["zpT"]))
        proj_cm(K_ZA, 2, epi_silu(S["zaT"]))

        cwv = sbt(k, es, "cwv", [128, 48], F32)
        cbv = sbt(k, es, "cbv", [128, 12], F32)
        load_cols(k, g, es, "cw", P["conv_w"][li].rearrange("k (i p) -> (k i) p", p=128), 48, cwv[:])
        load_cols(k, g, es, "cbias", P["conv_b"][li].rearrange("(i p) -> i p", p=128), 12, cbv[:])
        xp = [sbt(k, es, "xp%d" % i, [128, L + 16], F32) for i in range(2)]
        acc = Wk
        for i in range(2):
            k.op("pool", lambda e: e.memset(xp[i][:, 0:16], 0.0), writes=[k.R("xp", i)])

        def epi_conv(i, pa, pr, half):
            s = k.nxt("xp", 2)
            k.op("act", lambda e: e.copy(out=xp[s][:, 16:16 + L], in_=pa), reads=pr, writes=[k.R("xp", s)])
            k.op("act", lambda e: e.activation(out=acc[s][:, 0:L], in_=pa, func=AF.Copy, scale=cwv[:, 3 * 12 + i:3 * 12 + i + 1]),
                 reads=pr + [k.R("vec", "cw")], writes=[k.R("Wk", s)])
            for kk in (2, 1, 0):
                sh = 3 - kk
                k.op("dve", lambda e: e.scalar_tensor_tensor(out=acc[s][:, 0:L], in0=xp[s][:, 16 - sh:16 - sh + L],
                                                             scalar=cwv[:, kk * 12 + i:kk * 12 + i + 1], in1=acc[s][:, 0:L],
                                                             op0=ALU.mult, op1=ALU.add),
                     reads=[k.R("xp", s), k.R("Wk", s)], writes=[k.R("Wk", s)])
            s2 = k.nxt("stg", 2)
            k.op("act", lambda e: e.activation(out=stg[s2][:], in_=acc[s][:, 0:L], func=AF.Silu, bias=cbv[:, i:i + 1]),
                 reads=[k.R("Wk", s), k.R("vec", "cbias")], writes=[k.R("stg", s2)])
            k.dma("sp", S["xbcT"][i * 128:(i + 1) * 128, :], stg[s2][:], reads=[k.R("stg", s2)], sem="stg%d" % s2, store=True)

        proj_cm(K_XBC, 12, epi_conv)

        bufA, bufB = Wk
        pacc = sbt(k, es, "pacc", [128, L], F32)
        stgb = [sbt(k, es, "stgb%d" % i, [128, L], BF16) for i in range(2)]
        tmp16 = sbt(k, es, "tmp16", [128, 16], F32)
        acc16 = sbt(k, es, "acc16", [128, 16], F32)
        k.op("pool", lambda e: e.memset(bufA[:, 0:16], 0.0), writes=[k.R("Wk", 0)])
        k.op("pool", lambda e: e.memset(bufB[:, 0:16], 0.0), writes=[k.R("Wk", 1)])
        pc = g.poolc

        def epi_pool(i, pa, pr, half):
            gl = i // 4
            s = k.nxt("xp", 2)
            k.op("act", lambda e: e.copy(out=xp[s][:, 16:16 + L], in_=pa), reads=pr, writes=[k.R("xp", s)])
            src, srcR = xp[s], k.R("xp", s)
            bufs = [(bufA, k.R("Wk", 0)), (bufB, k.R("Wk", 1))]
            cand = (0, 2) if gl == 0 else (1, 3)
            first = True
            for j in range(cand[1] + 1):
                sh = 1 << j
                dst, dstR = bufs[j % 2]
                k.op("pool", lambda e: e.tensor_tensor(out=dst[:, 16:16 + L], in0=src[:, 16:16 + L], in1=src[:, 16 - sh:16 - sh + L],
                                                       op=ALU.add),
                     reads=[srcR], writes=[dstR])
                src, srcR = dst, dstR
                if j not in cand:
                    continue
                cj = pc[:, gl * 4 + j:gl * 4 + j + 1]
                rowj = pc[:, 8 + (gl * 4 + j) * 16: 8 + (gl * 4 + j) * 16 + 16]
                if first:
                    first = False
                    k.op("act", lambda e: e.activation(out=pacc[:], in_=src[:, 16:16 + L], func=AF.Copy, scale=cj),
                         reads=[srcR, k.R("poolc")], writes=[k.R("pacc")])
                    k.op("dve", lambda e: e.tensor_tensor(out=acc16[:], in0=src[:, 16:32], in1=rowj, op=ALU.mult),
                         reads=[srcR, k.R("poolc")], writes=[k.R("acc16")])
                else:
                    k.op("dve", lambda e: e.scalar_tensor_tensor(out=pacc[:], in0=src[:, 16:16 + L], scalar=cj, in1=pacc[:],
                                                                 op0=ALU.mult, op1=ALU.add),
                         reads=[srcR, k.R("poolc"), k.R("pacc")], writes=[k.R("pacc")])
                    k.op("dve", lambda e: e.tensor_tensor(out=tmp16[:], in0=src[:, 16:32], in1=rowj, op=ALU.mult),
                         reads=[srcR, k.R("poolc")], writes=[k.R("tmp16")])
                    k.op("dve", lambda e: e.tensor_tensor(out=acc16[:], in0=acc16[:], in1=tmp16[:], op=ALU.add),
                         reads=[k.R("tmp16"), k.R("acc16")], writes=[k.R("acc16")])
            sb_ = k.nxt("stgb", 2)
            k.op("dve", lambda e: e.tensor_tensor(out=stgb[sb_][:], in0=pacc[:], in1=xp[s][:, 16:16 + L], op=ALU.subtract),
                 reads=[k.R("pacc"), k.R("xp", s)], writes=[k.R("stgb", sb_)])
            k.op("dve", lambda e: e.tensor_tensor(out=stgb[sb_][:, 0:16], in0=acc16[:], in1=xp[s][:, 16:32], op=ALU.subtract),
                 reads=[k.R("acc16"), k.R("xp", s), k.R("stgb", sb_)], writes=[k.R("stgb", sb_)])
            k.dma("sp", S["poolT"][i * 128:(i + 1) * 128, :], stgb[sb_][:], reads=[k.R("stgb", sb_)], sem="stgb%d" % sb_, store=True)

        proj_cm(K_UP, 8, epi_pool)

        gq = sbt(k, es, "gq", [128, 2], F32)
        load_cols(k, g, es, "gqk", P["qk_g"][li], 2, gq[:])
        raw, rt = Wk[0][:, 0:L], Wk[1][:, 0:L]
        sqb = sbt(k, es, "sqb", [128, L], BF16)

        sqbs = [sqb, sbt(k, es, "sqb2", [128, L], BF16)]

        def epi_qk(i, pa, pr, half):
            isk = i // 6
            hl = i % 6
            d = DIL[hl // 2]
            sq = sqbs[i % 2]
            k.op("act", lambda e: e.activation(out=sq[:], in_=pa, func=AF.Square), reads=pr, writes=[k.R("sqb", i % 2)])
            k.op("act", lambda e: e.copy(out=raw, in_=pa), reads=pr, writes=[k.R("Wk", 0)])

            def deferred():
                for tg in range(4):
                    b = half * 4 + tg
                    k.op("pe", lambda e: e.matmul(out=psb(g, b), lhsT=g.cb[:, CO_ONE:CO_ONE + 128], rhs=sq[:, tg * 512:(tg + 1) * 512],
                                                  start=True, stop=True),
                         reads=[k.R("sqb", i % 2), k.R("cb")], writes=[k.R("ps", b)])
                k.op("act", lambda e: e.activation(out=rt, in_=pa, func=AF.Sqrt, scale=1.0 / 128, bias=g.epsb[:, 0:1]),
                     reads=pr + [k.R("epsb")], writes=[k.R("Wk", 1)])
                k.op("dve", lambda e: e.reciprocal(out=rt, in_=rt), reads=[k.R("Wk", 1)], writes=[k.R("Wk", 1)])
                sb_ = k.nxt("stgb", 2)
                outv = stgb[sb_][:].rearrange("p (r u) -> p u r", r=d)
                in0 = raw.rearrange("p (u r) -> p u r", r=d)
                in1 = rt.rearrange("p (u r) -> p u r", r=d)
                k.op("dve", lambda e: e.scalar_tensor_tensor(out=outv, in0=in0, scalar=gq[:, isk:isk + 1], in1=in1, op0=ALU.mult, op1=ALU.mult),
                     reads=[k.R("Wk", 0), k.R("Wk", 1), k.R("vec", "gqk")], writes=[k.R("stgb", sb_)])
                dstT = S["knT"] if isk else S["qnT"]
                k.dma("sp", dstT[hl * 128:(hl + 1) * 128, :], stgb[sb_][:], reads=[k.R("stgb", sb_)], sem="stgb%d" % sb_, store=True)
            return deferred

        proj_cm(K_Q, 12, epi_qk)

        def proj_tm(c0, cw, tokens_of, epi):
            s = wload(k, g, Wl, c0, cw)
            for tt in range(16):
                b = k.nxt("bank", 8)
                sl = tokens_of(tt)
                for kt in range(16):
                    k.op("pe", lambda e: e.matmul(out=psb(g, b, cw), lhsT=hT[:, kt, sl], rhs=g.wA[s][:, kt, 0:cw],
                                                  start=(kt == 0), stop=(kt == 15)),
                         reads=[k.R("wA", s)] + hT_all, writes=[k.R("ps", b)], inc=(kt == 15))
                epi(tt, b)

        nat = lambda tt: slice(tt * 128, (tt + 1) * 128)
        stqb = [sbt(k, es, "stqb%d" % i, [128, 256], BF16) for i in range(2)]
        for gi in range(3):
            d = DIL[gi]
            nb = 16 // d

            def tok(tt, d=d, nb=nb):
                r, jb = tt // nb, tt % nb
                st = r + d * jb * 128
                return slice(st, st + d * 127 + 1, d)

            def epi_v(tt, b, gi=gi):
                s = k.nxt("stqb", 2)
                k.op("act", lambda e: e.copy(out=stqb[s][:], in_=psb(g, b, 256)), reads=[k.R("ps", b)], writes=[k.R("stqb", s)])
                k.dma("sp", S["vTM"][gi, tt * 128:(tt + 1) * 128, :], stqb[s][:], reads=[k.R("stqb", s)], sem="stqb%d" % s, store=True)
            proj_tm(K_V + gi * 256, 256, tok, epi_v)

        dtb = sbt(k, es, "dtb", [128, 16], F32)
        k.dma("sp", dtb[:], P["dt_bias"][li].partition_broadcast(128), writes=[k.R("dtb")], sem="dtb")
        dtall = sbt(k, es, "dtall", [128, 16, 16], F32)
        dtt = sbt(k, es, "dtt", [128, 16], F32)

        def epi_dt(tt, b):
            k.op("dve", lambda e: e.tensor_tensor(out=dtt[:], in0=psb(g, b, 16), in1=dtb[:], op=ALU.add),
                 reads=[k.R("ps", b), k.R("dtb")], writes=[k.R("dtt")])
            k.op("act", lambda e: e.activation(out=dtt[:], in_=dtt[:], func=AF.Exp), reads=[k.R("dtt")], writes=[k.R("dtt")])
            k.op("act", lambda e: e.activation(out=dtall[:, tt, :], in_=dtt[:], func=AF.Ln, bias=g.oneb[:, 0:1]),
                 reads=[k.R("dtt"), k.R("epsb")], writes=[k.R("dtall")])
        proj_tm(K_DT, 16, nat, epi_dt)
        k.dma("sp", S["dtv"].rearrange("(tt p) h -> p tt h", p=128), dtall[:], reads=[k.R("dtall")], sem="dtall", store=True)
    k.barrier()


def ys_tile(S, i):
    return S["ysend%d" % (i // 3)][(i % 3) * 128:(i % 3 + 1) * 128, :]


def alloc_scratch(nc, kind="Internal"):
    S = {}

    def dt_(name, shape, dt):
        S[name] = nc.dram_tensor(name, shape, dt, kind=kind).ap()
    for c in range(2):
        dt_("hTb%d" % c, [1024, LT], BF16)
        dt_("hTg%d" % c, [2 * 1024, LT], BF16)
    dt_("zsT", [1024, L], F32)
    dt_("zpT", [1024, L], F32)
    dt_("zaT", [256, L], F32)
    dt_("xbcT", [1536, L], F32)
    dt_("poolT", [1024, L], BF16)
    dt_("qnT", [768, L], BF16)
    dt_("knT", [768, L], BF16)
    dt_("gates", [LT, 6144], F32)
    dt_("vTM", [3, L, 256], BF16)
    dt_("dtv", [L, 16], F32)
    for c in range(6):
        dt_("ysend%d" % c, [384, L], BF16)
        dt_("yg%d" % c, [2 * 384, L], BF16)
    dt_("ssp", [16, 128], F32)
    dt_("ssg", [32, 128], F32)
    dt_("xmid", [LT, D], F32)
    dt_("x1", [LT, D], F32)
    return S


PARAM_SHAPES = {
    "norm_g": [DEPTH, D], "w_in": [DEPTH, D, NCOL], "conv_w": [DEPTH, 4, 1536], "conv_b": [DEPTH, 1536],
    "dt_bias": [DEPTH, 16], "a_log": [DEPTH, 16], "d_skip": [DEPTH, 16], "ssd_norm_g": [DEPTH, 1024],
    "w_br": [DEPTH, 4608, D], "pool_w": [DEPTH, 2, 512, 512], "pool_scale": [DEPTH, 1024],
    "qk_g": [DEPTH, 2, 128], "w_out": [DEPTH, D, D], "ple_norm_g": [DEPTH, D], "w_ple_gate": [DEPTH, D, D],
    "w_ple_proj": [DEPTH, 256, D],
}


def setup_globals(k, es, g, consts_ap, sel_ap):
    load_consts(k, es, g, consts_ap)
    g.epsb = sbt(k, es, "epsb", [128, 1], F32)
    g.oneb = sbt(k, es, "oneb", [128, 1], F32)
    k.op("pool", lambda e: e.memset(g.epsb[:], EPS), writes=[k.R("epsb")])
    k.op("pool", lambda e: e.memset(g.oneb[:], 1.0), writes=[k.R("epsb")])
    g.poolc = sbt(k, es, "poolc", [128, 144], F32)
    k.dma("sp", g.poolc[:], sel_ap, writes=[k.R("poolc")], sem="poolc")


def phase_P(k, g, li, P, S):
    with ExitStack() as es:
        wP = sbt(k, es, "wP", [128, 4, 512], BF16)
        pl = sbt(k, es, "pl", [128, 4, L], BF16)
        zp = [sbt(k, es, "zp%d" % i, [128, L], F32) for i in range(2)]
        psc = sbt(k, es, "psc", [128, 8], F32)
        stgb = [sbt(k, es, "pstgb%d" % i, [128, L], BF16) for i in range(2)]
        load_cols(k, g, es, "psc", P["pool_scale"][li].rearrange("(i p) -> i p", p=128), 8, psc[:])
        for gi in range(2):
            k.dma("pool", wP[:], P["pool_w"][li, gi].rearrange("(kt p) c -> p kt c", p=128), writes=[k.R("wP")], sem="wP")
            k.dma("sp", pl[:], S["poolT"][gi * 512:(gi + 1) * 512, :].rearrange("(kt p) t -> p kt t", p=128),
                  writes=[k.R("pl")], sem="pl")
            for dt_ in range(4):
                i = gi * 4 + dt_
                s = k.nxt("zp", 2)
                k.dma("sp", zp[s][:], S["zpT"][i * 128:(i + 1) * 128, :], writes=[k.R("zp", s)], sem="zp%d" % s)
                half = k.nxt("half", 2)
                for tg in range(4):
                    b = half * 4 + tg
                    for kt in range(4):
                        k.op("pe", lambda e: e.matmul(out=psb(g, b), lhsT=wP[:, kt, dt_ * 128:(dt_ + 1) * 128],
                                                      rhs=pl[:, kt, tg * 512:(tg + 1) * 512], start=(kt == 0), stop=(kt == 3)),
                             reads=[k.R("wP"), k.R("pl")], writes=[k.R("ps", b)], inc=(kt == 3))
                pa = g.ps[:, half * 2048:(half + 1) * 2048]
                pr = [k.R("ps", half * 4 + t) for t in range(4)]
                sb_ = k.nxt("pstgb", 2)
                k.op("dve", lambda e: e.scalar_tensor_tensor(out=stgb[sb_][:], in0=pa, scalar=psc[:, i:i + 1], in1=zp[s][:],
                                                             op0=ALU.mult, op1=ALU.mult),
                     reads=pr + [k.R("vec", "psc"), k.R("zp", s)], writes=[k.R("pstgb", sb_)])
                k.dma("sp", ys_tile(S, 8 + i), stgb[sb_][:], reads=[k.R("pstgb", sb_)], sem="pstgb%d" % sb_, store=True)
    k.barrier()


def phase_AT(k, g, li, P, S):
    with ExitStack() as es:
        gqb = sbt(k, es, "gqb", [128, 256], F32)
        mx = sbt(k, es, "mx", [128, 4], F32)
        k.dma("sp", gqb[:], P["qk_g"][li].rearrange("a c -> (a c)").partition_broadcast(128), writes=[k.R("gqb")], sem="gqb")
        gq2 = sbt(k, es, "gq2", [128, 256], F32)
        k.op("dve", lambda e: e.tensor_scalar(out=gq2[:], in0=gqb[:], scalar1=-1.0, scalar2=None, op0=ALU.mult), reads=[k.R("gqb")], writes=[k.R("gq2")])
        k.op("dve", lambda e: e.tensor_tensor(out=gqb[:], in0=gqb[:], in1=gq2[:], op=ALU.max), reads=[k.R("gqb"), k.R("gq2")], writes=[k.R("gqb")])
        k.op("dve", lambda e: e.tensor_reduce(out=mx[:, 0:1], in_=gqb[:, 0:128], axis=AX.X, op=ALU.max), reads=[k.R("gqb")], writes=[k.R("mx")])
        k.op("dve", lambda e: e.tensor_reduce(out=mx[:, 1:2], in_=gqb[:, 128:256], axis=AX.X, op=ALU.max), reads=[k.R("gqb"), k.R("mx")], writes=[k.R("mx")])
        k.op("dve", lambda e: e.tensor_tensor(out=mx[:, 2:3], in0=mx[:, 0:1], in1=mx[:, 1:2], op=ALU.mult), reads=[k.R("mx")], writes=[k.R("mx")])
        k.op("dve", lambda e: e.tensor_scalar(out=mx[:, 3:4], in0=mx[:, 2:3], scalar1=-float(np.sqrt(128.0)), scalar2=None, op0=ALU.mult),
             reads=[k.R("mx")], writes=[k.R("mx")])
        negM = mx[:, 3:4]
        qn = [sbt(k, es, "qn%d" % i, [128, L], BF16) for i in range(3)]
        kn = [sbt(k, es, "kn%d" % i, [128, L], BF16) for i in range(3)]
        vt = [sbt(k, es, "vt%d" % i, [128, 16, 128], BF16) for i in range(3)]
        acc = sbt(k, es, "aacc", [128, 2, L], F32)
        E = [sbt(k, es, "E%d" % i, [128, 256], BF16) for i in range(6)]
        za = sbt(k, es, "za", [128, L], F32)
        rec = sbt(k, es, "rec", [128, L], F32)
        yb = sbt(k, es, "yb", [128, L], BF16)
        sc = float(1.0 / np.sqrt(128.0))
        for j in range(2):
            for gi in range(3):
                h = gi * 2 + j
                k.dma("sp", qn[gi][:], S["qnT"][h * 128:(h + 1) * 128, :], writes=[k.R("qn", gi)], sem="qn%d" % gi)
                k.dma("sp", kn[gi][:], S["knT"][h * 128:(h + 1) * 128, :], writes=[k.R("kn", gi)], sem="kn%d" % gi)
                k.dma("sp", vt[gi][:], S["vTM"][gi, :, j * 128:(j + 1) * 128].rearrange("(blk p) c -> p blk c", p=128),
                      writes=[k.R("vt", gi)], sem="vt%d" % gi)
            k.dma("sp", za[:], S["zaT"][j * 128:(j + 1) * 128, :], writes=[k.R("za")], sem="za")
            for gi in range(3):
                d = DIL[gi]
                nb = 16 // d
                rd = [k.R("qn", gi), k.R("kn", gi)]
                pendA = []

                def finishA(item):
                    blk, e_, hp, gi_, d_, nb_ = item
                    r, jb = blk // nb_, blk % nb_
                    b2 = k.nxt("bank", 8)
                    rv = [k.R("vt", gi_), k.R("E", e_), k.R("cb")]
                    k.op("pe", lambda e: e.matmul(out=psb(g, b2, 128, 0), lhsT=vt[gi_][:, blk, :], rhs=E[e_][:, 0:128], start=True, stop=not hp),
                         reads=rv, writes=[k.R("ps", b2)], inc=False)
                    if hp:
                        k.op("pe", lambda e: e.matmul(out=psb(g, b2, 128, 0), lhsT=vt[gi_][:, blk - 1, :], rhs=E[e_][:, 128:256], start=False, stop=True),
                             reads=rv, writes=[k.R("ps", b2)], inc=False)
                    k.op("pe", lambda e: e.matmul(out=psb(g, b2, 128, 128), lhsT=g.cb[:, CO_ONE:CO_ONE + 128], rhs=E[e_][:, 0:128], start=True, stop=not hp),
                         reads=rv, writes=[k.R("ps", b2)], inc=not hp)
                    if hp:
                        k.op("pe", lambda e: e.matmul(out=psb(g, b2, 128, 128), lhsT=g.cb[:, CO_ONE:CO_ONE + 128], rhs=E[e_][:, 128:256], start=False, stop=True),
                             reads=rv, writes=[k.R("ps", b2)])
                    st = r + d_ * jb * 128
                    accv = acc[:, :, st:st + d_ * 127 + 1:d_]
                    psv = psb(g, b2, 256).rearrange("p (a c) -> p a c", a=2)
                    if gi_ == 0:
                        k.op("act", lambda e: e.copy(out=accv, in_=psv), reads=[k.R("ps", b2)], writes=[k.R("aacc")])
                    else:
                        k.op("dve", lambda e: e.tensor_tensor(out=accv, in0=accv, in1=psv, op=ALU.add), reads=[k.R("ps", b2), k.R("aacc")], writes=[k.R("aacc")])
                for blk in range(16):
                    r, jb = blk // nb, blk % nb
                    hp = jb > 0
                    wd = 256 if hp else 128
                    b = k.nxt("bank", 8)
                    qs = qn[gi][:, blk * 128:(blk + 1) * 128]
                    k.op("pe", lambda e: e.matmul(out=psb(g, b, 128, 0), lhsT=kn[gi][:, blk * 128:(blk + 1) * 128], rhs=qs, start=True, stop=True),
                         reads=rd, writes=[k.R("ps", b)], inc=not hp)
                    if hp:
                        k.op("pe", lambda e: e.matmul(out=psb(g, b, 128, 128), lhsT=kn[gi][:, (blk - 1) * 128:blk * 128], rhs=qs, start=True, stop=True),
                             reads=rd, writes=[k.R("ps", b)])
                    e_ = k.nxt("E", 6)
                    k.op("act", lambda e: e.activation(out=E[e_][:, 0:wd], in_=psb(g, b, wd), func=AF.Exp, scale=sc, bias=negM),
                         reads=[k.R("ps", b), k.R("mx")], writes=[k.R("E", e_)])
                    k.op("dve", lambda e: e.tensor_tensor(out=E[e_][:, 0:wd], in0=E[e_][:, 0:wd], in1=g.cb[:, CO_LE:CO_LE + wd], op=ALU.mult),
                         reads=[k.R("E", e_), k.R("cb")], writes=[k.R("E", e_)])
                    pendA.append((blk, e_, hp, gi, d, nb))
                    if len(pendA) > 3:
                        finishA(pendA.pop(0))
                while pendA:
                    finishA(pendA.pop(0))
            k.op("dve", lambda e: e.reciprocal(out=rec[:], in_=acc[:, 1, :]), reads=[k.R("aacc")], writes=[k.R("rec")])
            k.op("dve", lambda e: e.tensor_tensor(out=rec[:], in0=rec[:], in1=acc[:, 0, :], op=ALU.mult), reads=[k.R("aacc"), k.R("rec")], writes=[k.R("rec")])
            k.op("dve", lambda e: e.tensor_tensor(out=yb[:], in0=rec[:], in1=za[:], op=ALU.mult), reads=[k.R("rec"), k.R("za")], writes=[k.R("yb")])
            k.dma("sp", ys_tile(S, 16 + j), yb[:], reads=[k.R("yb")], sem="yb", store=True)
    k.barrier()


def phase_S(k, g, li, P, S):
    with ExitStack() as es:
        alb = sbt(k, es, "alb", [128, 16], F32)
        dskb = sbt(k, es, "dskb", [128, 16], F32)
        dsk = sbt(k, es, "dsk", [128, 8], F32)
        sng = sbt(k, es, "sng", [128, 8], F32)
        k.dma("sp", alb[:], P["a_log"][li].partition_broadcast(128), writes=[k.R("alb")], sem="alb")
        k.op("act", lambda e: e.activation(out=alb[:], in_=alb[:], func=AF.Exp), reads=[k.R("alb")], writes=[k.R("alb")])
        k.op("dve", lambda e: e.tensor_scalar(out=alb[:], in0=alb[:], scalar1=-1.0, scalar2=None, op0=ALU.mult), reads=[k.R("alb")], writes=[k.R("alb")])
        k.dma("sp", dskb[:], P["d_skip"][li].partition_broadcast(128), writes=[k.R("dskb")], sem="dskb")
        k.op("dve", lambda e: e.tensor_copy(out=dsk[0:64, :], in_=dskb[0:64, 0:16:2]), reads=[k.R("dskb")], writes=[k.R("dsk")])
        k.op("dve", lambda e: e.tensor_copy(out=dsk[64:128, :], in_=dskb[64:128, 1:16:2]), reads=[k.R("dskb"), k.R("dsk")], writes=[k.R("dsk")])
        load_cols(k, g, es, "sng", P["ssd_norm_g"][li].rearrange("(i p) -> i p", p=128), 8, sng[:])
        Sin = sbt(k, es, "Sin", [128, 2, 512], F32)
        Sbf = sbt(k, es, "Sbf", [128, 2, 512], BF16)
        k.op("dve", lambda e: e.memset(Sin[:], 0.0), writes=[k.R("Sin")])
        k.op("dve", lambda e: e.memset(Sbf[:], 0.0), writes=[k.R("Sbf")])
        xc = [sbt(k, es, "xc%d" % i, [128, 12, 128], F32) for i in range(2)]
        zc = [sbt(k, es, "zc%d" % i, [128, 8, 128], F32) for i in range(2)]
        dtc = [sbt(k, es, "dtc%d" % i, [128, 16], F32) for i in range(2)]
        da_ = [sbt(k, es, "da%d" % i, [128, 16], F32) for i in range(2)]
        acs_ = [sbt(k, es, "acs%d" % i, [128, 32], F32) for i in range(2)]
        wdec_ = [sbt(k, es, "wdec%d" % i, [128, 16], F32) for i in range(2)]
        etot_ = [sbt(k, es, "etot%d" % i, [128, 16], F32) for i in range(2)]
        xh_ = [sbt(k, es, "xh%d" % i, [128, 1024], BF16) for i in range(2)]
        xhd_ = [sbt(k, es, "xhd%d" % i, [128, 1024], BF16) for i in range(2)]
        Btm_ = [sbt(k, es, "Btm%d" % i, [128, 2, 128], BF16) for i in range(2)]
        bcb_ = [sbt(k, es, "bcb%d" % i, [128, 4, 128], BF16) for i in range(2)]
        cbm_l = [sbt(k, es, "cbm%d" % i, [128, 128], F32) for i in range(2)]
        rhsh = [sbt(k, es, "rhsh%d" % i, [128, 2, 128], F32) for i in range(4)]
        ex = [sbt(k, es, "ex%d" % i, [128, 2, 256], F32) for i in range(4)]
        MT = [sbt(k, es, "MT%d" % i, [128, 2, 128], BF16) for i in range(4)]
        Cp = [sbt(k, es, "Cp%d" % i, [128, 2, 128], BF16) for i in range(4)]
        yt__l = [sbt(k, es, "yt_%d" % i, [128, 4, 128], F32) for i in range(2)]
        ygz_ = [sbt(k, es, "ygz%d" % i, [128, 8, 128], F32) for i in range(2)]
        sqy_l = [sbt(k, es, "sqy%d" % i, [128, 512], BF16) for i in range(2)]
        sscol = sbt(k, es, "sscol", [128, 16], F32)
        ynb = [sbt(k, es, "ynb%d" % i, [128, 8, 128], BF16) for i in range(2)]
        LE = g.cf[:, CO_LE:CO_LE + 128]
        GT = g.cf[:, CO_GT:CO_GT + 128]
        ONEF = g.cf[:, CO_ONE:CO_ONE + 128]
        ONEB = g.cb[:, CO_ONE:CO_ONE + 128]
        pend_tail = []
        for c in range(16):
            s = c % 2
            tsl = slice(c * 128, (c + 1) * 128)
            k.dma("sp", xc[s][:], S["xbcT"][:, tsl].rearrange("(i p) t -> p i t", p=128), writes=[k.R("xc", s)], sem="xc%d" % s)
            k.dma("sp", zc[s][:], S["zsT"][:, tsl].rearrange("(i p) t -> p i t", p=128), writes=[k.R("zc", s)], sem="zc%d" % s)
            k.dma("sp", dtc[s][:], S["dtv"][tsl, :], writes=[k.R("dtc", s)], sem="dtc%d" % s)
            X, Z, DT = xc[s], zc[s], dtc[s]
            da, acs, wdec, etot, xh, xhd, Btm, bcb, ygz = da_[s], acs_[s], wdec_[s], etot_[s], xh_[s], xhd_[s], Btm_[s], bcb_[s], ygz_[s]
            rx, rz, rdt = k.R("xc", s), k.R("zc", s), k.R("dtc", s)
            k.op("dve", lambda e: e.tensor_tensor(out=da[:], in0=DT[:], in1=alb[:], op=ALU.mult), reads=[rdt, k.R("alb")], writes=[k.R("da", s)])
            b = k.nxt("bank", 8)
            k.op("pe", lambda e: e.matmul(out=psb(g, b, 16, 0), lhsT=LE, rhs=da[:], start=True, stop=True), reads=[k.R("da", s), k.R("cf")], writes=[k.R("ps", b)], inc=False)
            k.op("pe", lambda e: e.matmul(out=psb(g, b, 16, 16), lhsT=ONEF, rhs=da[:], start=True, stop=True), reads=[k.R("da", s), k.R("cf")], writes=[k.R("ps", b)])
            k.op("dve", lambda e: e.tensor_copy(out=acs[:], in_=psb(g, b, 32)), reads=[k.R("ps", b)], writes=[k.R("acs", s)])
            k.op("dve", lambda e: e.tensor_tensor(out=wdec[:], in0=acs[:, 16:32], in1=acs[:, 0:16], op=ALU.subtract), reads=[k.R("acs", s)], writes=[k.R("wdec", s)])
            k.op("act", lambda e: e.activation(out=wdec[:], in_=wdec[:], func=AF.Exp), reads=[k.R("wdec", s)], writes=[k.R("wdec", s)])
            k.op("dve", lambda e: e.tensor_tensor(out=wdec[:], in0=wdec[:], in1=DT[:], op=ALU.mult), reads=[k.R("wdec", s), rdt], writes=[k.R("wdec", s)])
            k.op("act", lambda e: e.activation(out=etot[:], in_=acs[:, 16:32], func=AF.Exp), reads=[k.R("acs", s)], writes=[k.R("etot", s)])
            for b4 in range(2):
                b = k.nxt("bank", 8)
                for j in range(4):
                    i = b4 * 4 + j
                    k.op("pe", lambda e: e.transpose(out=psb(g, b, 128, j * 128), in_=X[:, i, :], identity=g.ident),
                         reads=[rx, k.R("cf")], writes=[k.R("ps", b)], inc=(j == 3))
                pv = psb(g, b).rearrange("p (h q) -> p h q", h=8)
                k.op("dve", lambda e: e.tensor_tensor(out=xh[:, b4 * 512:(b4 + 1) * 512].rearrange("p (h q) -> p h q", h=8), in0=pv,
                                                      in1=DT[:, b4 * 8:(b4 + 1) * 8].unsqueeze(2).to_broadcast([128, 8, 64]), op=ALU.mult),
                     reads=[k.R("ps", b), rdt], writes=[k.R("xh", s)])
                k.op("dve", lambda e: e.tensor_tensor(out=xhd[:, b4 * 512:(b4 + 1) * 512].rearrange("p (h q) -> p h q", h=8), in0=pv,
                                                      in1=wdec[:, b4 * 8:(b4 + 1) * 8].unsqueeze(2).to_broadcast([128, 8, 64]), op=ALU.mult),
                     reads=[k.R("ps", b), k.R("wdec", s)], writes=[k.R("xhd", s)])
            b = k.nxt("bank", 8)
            for j in range(2):
                k.op("pe", lambda e: e.transpose(out=psb(g, b, 128, j * 128), in_=X[:, 8 + j, :], identity=g.ident),
                     reads=[rx, k.R("cf")], writes=[k.R("ps", b)], inc=(j == 1))
            k.op("act", lambda e: e.copy(out=Btm[:], in_=psb(g, b, 256).rearrange("p (a c) -> p a c", a=2)), reads=[k.R("ps", b)], writes=[k.R("Btm", s)])
            k.op("act", lambda e: e.copy(out=bcb[:], in_=X[:, 8:12, :]), reads=[rx], writes=[k.R("bcb", s)])
            if pend_tail:
                pend_tail.pop(0)()
            bss = k.nxt("bank", 8)
            resv = [bss]

            def freebank():
                b_ = k.nxt("bank", 8)
                while b_ in resv:
                    b_ = k.nxt("bank", 8)
                return b_
            bys = []
            for gq in range(2):
                by = freebank()
                resv.append(by)
                bys.append(by)
            for gq in range(2):
                b = freebank()
                k.op("pe", lambda e: e.matmul(out=psb(g, b, 128), lhsT=bcb[:, gq, :], rhs=bcb[:, 2 + gq, :], start=True, stop=True),
                     reads=[k.R("bcb", s)], writes=[k.R("ps", b)])
                k.op("dve", lambda e: e.tensor_tensor(out=cbm_l[gq][:], in0=psb(g, b, 128), in1=LE, op=ALU.mult), reads=[k.R("ps", b), k.R("cf")], writes=[k.R("cbm", gq)])

            def stageA(pi):
                t2 = pi % 4
                h0 = 2 * pi
                k.op("dve", lambda e: e.tensor_tensor(out=rhsh[t2][:], in0=LE.unsqueeze(1).to_broadcast([128, 2, 128]),
                                                      in1=da[:, h0:h0 + 2].unsqueeze(2).to_broadcast([128, 2, 128]), op=ALU.mult),
                     reads=[k.R("cf"), k.R("da", s)], writes=[k.R("rhsh", t2)])
                b = freebank()
                for hh in range(2):
                    k.op("pe", lambda e: e.matmul(out=psb(g, b, 128, hh * 256), lhsT=GT, rhs=rhsh[t2][:, hh, :], start=True, stop=True),
                         reads=[k.R("rhsh", t2), k.R("cf")], writes=[k.R("ps", b)], inc=False)
                    k.op("pe", lambda e: e.matmul(out=psb(g, b, 128, hh * 256 + 128), lhsT=ONEF, rhs=rhsh[t2][:, hh, :], start=True, stop=True),
                         reads=[k.R("rhsh", t2), k.R("cf")], writes=[k.R("ps", b)], inc=(hh == 1))
                k.op("act", lambda e: e.activation(out=ex[t2][:].rearrange("p a c -> p (a c)"), in_=psb(g, b, 512), func=AF.Exp),
                     reads=[k.R("ps", b)], writes=[k.R("ex", t2)])

            def stageB(pi):
                t2 = pi % 4
                gq = pi // 4
                by = bys[gq]
                k.op("dve", lambda e: e.tensor_tensor(out=MT[t2][:], in0=ex[t2][:, :, 0:128], in1=cbm_l[gq][:].unsqueeze(1).to_broadcast([128, 2, 128]), op=ALU.mult),
                     reads=[k.R("ex", t2), k.R("cbm", gq)], writes=[k.R("MT", t2)])
                k.op("dve", lambda e: e.tensor_tensor(out=Cp[t2][:], in0=ex[t2][:, :, 128:256], in1=X[:, 10 + gq, :].unsqueeze(1).to_broadcast([128, 2, 128]), op=ALU.mult),
                     reads=[k.R("ex", t2), rx], writes=[k.R("Cp", t2)])
                for hh in range(2):
                    h = 2 * pi + hh
                    r = h % 8
                    po = (r % 2) * 64
                    yout = g.ps[po:po + 64, by * 512 + (r // 2) * 128: by * 512 + (r // 2) * 128 + 128]
                    k.op("pe", lambda e: e.matmul(out=yout, lhsT=xh[:, h * 64:(h + 1) * 64], rhs=MT[t2][:, hh, :], start=True, stop=False),
                         reads=[k.R("xh", s), k.R("MT", t2)], writes=[k.R("ps", by)], inc=False)
                    k.op("pe", lambda e: e.matmul(out=yout, lhsT=Sbf[:, gq, r * 64:(r + 1) * 64], rhs=Cp[t2][:, hh, :], start=False, stop=True),
                         reads=[k.R("Sbf"), k.R("Cp", t2)], writes=[k.R("ps", by)], inc=(hh == 1))

            def epilogue(gq):
                by = bys[gq]
                yt_, sqy = yt__l[gq], sqy_l[gq]
                for pr_ in range(4):
                    i = gq * 4 + pr_
                    k.op("dve", lambda e: e.scalar_tensor_tensor(out=yt_[:, pr_, :], in0=X[:, i, :], scalar=dsk[:, i:i + 1],
                                                                 in1=psb(g, by, 128, pr_ * 128), op0=ALU.mult, op1=ALU.add),
                         reads=[rx, k.R("dsk"), k.R("ps", by)], writes=[k.R("yt_", gq)])
                k.op("dve", lambda e: e.tensor_tensor(out=ygz[:, gq * 4:(gq + 1) * 4, :], in0=yt_[:], in1=Z[:, gq * 4:(gq + 1) * 4, :], op=ALU.mult),
                     reads=[k.R("yt_", gq), rz], writes=[k.R("ygz", s)])
                k.op("act", lambda e: e.activation(out=sqy[:].rearrange("p (a c) -> p a c", a=4), in_=ygz[:, gq * 4:(gq + 1) * 4, :], func=AF.Square),
                     reads=[k.R("ygz", s)], writes=[k.R("sqy", gq)])
                for pr_ in range(4):
                    i = gq * 4 + pr_
                    k.op("pe", lambda e: e.matmul(out=psb(g, bss, 1), lhsT=sqy[:, pr_ * 128:(pr_ + 1) * 128], rhs=ONEB[:, 0:1], start=(i == 0), stop=(i == 7)),
                         reads=[k.R("sqy", gq), k.R("cb")], writes=[k.R("ps", bss)], inc=(pr_ == 3))
                b = freebank()
                k.op("pe", lambda e: e.matmul(out=psb(g, b), lhsT=Btm[:, gq, :], rhs=xhd[:, gq * 512:(gq + 1) * 512], start=True, stop=True),
                     reads=[k.R("Btm", s), k.R("xhd", s)], writes=[k.R("ps", b)])
                sv = Sin[:, gq, :].rearrange("p (h q) -> p h q", h=8)
                k.op("dve", lambda e: e.tensor_tensor(out=sv, in0=sv, in1=etot[:, gq * 8:(gq + 1) * 8].unsqueeze(2).to_broadcast([128, 8, 64]), op=ALU.mult),
                     reads=[k.R("Sin"), k.R("etot", s)], writes=[k.R("Sin")])
                k.op("dve", lambda e: e.tensor_tensor(out=Sin[:, gq, :], in0=Sin[:, gq, :], in1=psb(g, b), op=ALU.add),
                     reads=[k.R("Sin"), k.R("ps", b)], writes=[k.R("Sin")])
                k.op("act", lambda e: e.copy(out=Sbf[:, gq, :], in_=Sin[:, gq, :]), reads=[k.R("Sin")], writes=[k.R("Sbf")])

            LA = 2
            for idx in range(8 + LA):
                if idx < 8:
                    stageA(idx)
                if idx >= LA:
                    pi = idx - LA
                    stageB(pi)
                    if pi % 4 == 3:
                        epilogue(pi // 4)
            k.op("dve", lambda e: e.tensor_copy(out=sscol[:, c:c + 1], in_=psb(g, bss, 1)), reads=[k.R("ps", bss)], writes=[k.R("sscol")])
            def tail(s=s, ygz=ygz, tsl=tsl):
                yn = ynb[s]
                for i in range(8):
                    k.op("dve", lambda e: e.tensor_scalar(out=yn[:, i, :], in0=ygz[:, i, :], scalar1=sng[:, i:i + 1], scalar2=None, op0=ALU.mult),
                         reads=[k.R("ygz", s), k.R("vec", "sng")], writes=[k.R("ynb", s)])
                for (i0, i1) in ((0, 3), (3, 6), (6, 8)):
                    k.dma("sp", S["ysend%d" % (i0 // 3)][0:(i1 - i0) * 128, tsl].rearrange("(i p) t -> p i t", p=128), yn[:, i0:i1, :],
                          reads=[k.R("ynb", s)], sem="ynb%d" % s, store=True)
            pend_tail.append(tail)
        while pend_tail:
            pend_tail.pop(0)()
        b = k.nxt("bank", 8)
        sst = sbt(k, es, "sst", [16, 128], F32)
        k.op("pe", lambda e: e.transpose(out=g.ps[0:16, b * 512:b * 512 + 128], in_=sscol[:], identity=g.ident),
             reads=[k.R("sscol"), k.R("cf")], writes=[k.R("ps", b)])
        k.op("dve", lambda e: e.tensor_copy(out=sst[:], in_=g.ps[0:16, b * 512:b * 512 + 128]), reads=[k.R("ps", b)], writes=[k.R("sst")])
        k.dma("sp", S["ssp"], sst[:], reads=[k.R("sst")], sem="sst", store=True)
    k.barrier()


def phase_T(k, g, li, x_src, x_dst, p_src, P, S):
    s0 = g.poolc[:, 136:137]
    s1 = g.poolc[:, 137:138]
    with ExitStack() as es:
        mT = g.mT
        with ExitStack() as e1:
            Ysel = g.Ysel
            wB = [sbt(k, e1, "wB%d" % i, [128, 36, 512], BF16) for i in range(2)]
            gt = [sbt(k, e1, "gt%d" % i, [128, 3, 512], F32) for i in range(2)]
            m0 = sbt(k, e1, "m0", [128, 512], F32)
            m1 = sbt(k, e1, "m1", [128, 512], F32)
            ssq = sbt(k, e1, "ssq", [128, 32], F32)
            rs = sbt(k, e1, "rs", [128, 16], F32)
            load_cols(k, g, e1, "ssg", S["ssg"], 32, ssq[:])
            k.op("dve", lambda e: e.tensor_tensor(out=ssq[:, 0:16], in0=ssq[:, 0:16], in1=ssq[:, 16:32], op=ALU.add), reads=[k.R("vec", "ssg")], writes=[k.R("vec", "ssg")])
            k.op("dve", lambda e: e.tensor_scalar(out=rs[:, 0:8], in0=ssq[:, 0:8], scalar1=s0, scalar2=None, op0=ALU.mult),
                 reads=[k.R("vec", "ssg"), k.R("poolc")], writes=[k.R("rs")])
            k.op("dve", lambda e: e.scalar_tensor_tensor(out=rs[:, 0:8], in0=ssq[:, 8:16], scalar=s1, in1=rs[:, 0:8], op0=ALU.mult, op1=ALU.add),
                 reads=[k.R("vec", "ssg"), k.R("poolc"), k.R("rs")], writes=[k.R("rs")])
            k.op("act", lambda e: e.activation(out=rs[:, 8:16], in_=rs[:, 0:8], func=AF.Sqrt, scale=1.0 / 2048, bias=g.epsb[:, 0:1]),
                 reads=[k.R("rs"), k.R("epsb")], writes=[k.R("rs")])
            k.op("dve", lambda e: e.reciprocal(out=rs[:, 0:8], in_=rs[:, 8:16]), reads=[k.R("rs")], writes=[k.R("rs")])
            ysel_build(k, g, S, range(15, 18))
            Yall = [k.R("Ysel", kt) for kt in range(36)]
            branch = [(0 if i < 8 else (1 if i < 16 else 2)) for i in range(18)] * 2
            m0s = [m0, sbt(k, e1, "m0b", [128, 512], F32)]
            m1s = [m1, m1]
            pend = []

            def finish(item):
                par, dg_, tt_ = item
                mm = m0s[par]
                b = k.nxt("bank", 8)
                for j in range(4):
                    k.op("pe", lambda e: e.transpose(out=psb(g, b, 128, j * 128), in_=mm[:, j * 128:(j + 1) * 128], identity=g.ident),
                         reads=[k.R("m0", par), k.R("cf")], writes=[k.R("ps", b)], inc=(j == 3))
                k.op("act", lambda e: e.copy(out=mT[:, dg_ * 4:(dg_ + 1) * 4, tt_ * 128:(tt_ + 1) * 128], in_=psb(g, b).rearrange("p (a c) -> p a c", a=4)),
                     reads=[k.R("ps", b)], writes=[k.R("mT", tt_)])
            for dg in range(4):
                ws = dg % 2
                csl = slice(dg * 512, (dg + 1) * 512)
                k.dma("pool", wB[ws][:, 0:18, :], P["w_br"][li][0:2304, csl].rearrange("(kt p) c -> p kt c", p=128), writes=[k.R("wB", ws)], sem="wB%d" % ws)
                k.dma("pool", wB[ws][:, 18:36, :], P["w_br"][li][2304:4608, csl].rearrange("(kt p) c -> p kt c", p=128), writes=[k.R("wB", ws)], sem="wB%d" % ws)
                for tt in range(NT):
                    s = k.nxt("gt", 2)
                    par = k.nxt("m0par", 2)
                    mm0, mm1 = m0s[par], m1s[par]
                    tsl = slice(tt * 128, (tt + 1) * 128)
                    k.dma("sp", gt[s][:], S["gates"][tsl, :].rearrange("p (a c) -> p a c", a=3)[:, :, csl], writes=[k.R("gt", s)], sem="gt%d" % s)
                    bs = [k.nxt("bank", 8) for _ in range(3)]
                    for br in range(3):
                        idx = [i for i in range(36) if branch[i] == br]
                        for n_, i in enumerate(idx):
                            k.op("pe", lambda e: e.matmul(out=psb(g, bs[br]), lhsT=Ysel[:, i, tsl], rhs=wB[ws][:, i, :], start=(n_ == 0), stop=(n_ == len(idx) - 1)),
                                 reads=[Yall[i], k.R("wB", ws)], writes=[k.R("ps", bs[br])], inc=(n_ == len(idx) - 1))
                    if pend:
                        finish(pend.pop(0))
                    k.op("dve", lambda e: e.scalar_tensor_tensor(out=mm0[:], in0=psb(g, bs[0]), scalar=rs[:, tt:tt + 1], in1=gt[s][:, 0, :], op0=ALU.mult, op1=ALU.mult),
                         reads=[k.R("ps", bs[0]), k.R("gt", s), k.R("rs")], writes=[k.R("m0", par)])
                    k.op("dve", lambda e: e.tensor_tensor(out=mm1[:], in0=psb(g, bs[1]), in1=gt[s][:, 1, :], op=ALU.mult), reads=[k.R("ps", bs[1]), k.R("gt", s)], writes=[k.R("m1")])
                    k.op("dve", lambda e: e.tensor_tensor(out=mm0[:], in0=mm0[:], in1=mm1[:], op=ALU.add), reads=[k.R("m0", par), k.R("m1")], writes=[k.R("m0", par)])
                    k.op("dve", lambda e: e.tensor_tensor(out=mm1[:], in0=psb(g, bs[2]), in1=gt[s][:, 2, :], op=ALU.mult), reads=[k.R("ps", bs[2]), k.R("gt", s), k.R("m1")], writes=[k.R("m1")])
                    k.op("dve", lambda e: e.tensor_tensor(out=mm0[:], in0=mm0[:], in1=mm1[:], op=ALU.add), reads=[k.R("m0", par), k.R("m1")], writes=[k.R("m0", par)])
                    pend.append((par, dg, tt))
            while pend:
                finish(pend.pop(0))
            k.barrier()
        g.el.close()
        with ExitStack() as e2:
            wO = sbt(k, e2, "wO", [128, 16, D], BF16)
            xt = [sbt(k, e2, "txt%d" % i, [128, D], F32) for i in range(2)]
            xn = sbt(k, e2, "txn", [128, D], F32)
            gB2 = sbt(k, e2, "gB2", [128, D], F32)
            junk = sbt(k, e2, "tjunk", [128, D], BF16)
            stat = sbt(k, e2, "tstat", [128, 48], F32)
            for cg in range(4):
                k.dma("pool", wO[:, :, cg * 512:(cg + 1) * 512], P["w_out"][li][:, cg * 512:(cg + 1) * 512].rearrange("(kt p) c -> p kt c", p=128),
                      writes=[k.R("wO", cg)], sem="wO%d" % cg)
            k.dma("sp", gB2[:], P["ple_norm_g"][li].partition_broadcast(128), writes=[k.R("gB2")], sem="gB2")
            xns = [xn, sbt(k, e2, "txnb", [128, D], F32)]
            pend2 = []

            def finish2(item):
                tt_, s_, half_ = item
                for b4 in range(4):
                    b = half_ * 4 + b4
                    for j in range(4):
                        kt = b4 * 4 + j
                        k.op("pe", lambda e: e.transpose(out=psb(g, b, 128, j * 128), in_=xns[s_][:, kt * 128:(kt + 1) * 128], identity=g.ident),
                             reads=[k.R("txn", s_), k.R("cf")], writes=[k.R("ps", b)], inc=(j == 3))
                    k.op("act", lambda e: e.copy(out=mT[:, b4 * 4:(b4 + 1) * 4, tt_ * 128:(tt_ + 1) * 128], in_=psb(g, b).rearrange("p (a c) -> p a c", a=4)),
                         reads=[k.R("ps", b)], writes=[k.R("mT", tt_)])
            for tt in range(NT):
                s = tt % 2
                tsl = slice(tt * 128, (tt + 1) * 128)
                k.dma("sp", xt[s][:], x_src[tsl, :], writes=[k.R("txt", s)], sem="txt%d" % s)
                half = k.nxt("half", 2)
                for eg in range(4):
                    b = half * 4 + eg
                    for kt in range(16):
                        k.op("pe", lambda e: e.matmul(out=psb(g, b), lhsT=mT[:, kt, tsl], rhs=wO[:, kt, eg * 512:(eg + 1) * 512], start=(kt == 0), stop=(kt == 15)),
                             reads=[k.R("mT", tt), k.R("wO", eg)], writes=[k.R("ps", b)], inc=(kt == 15))
                if pend2:
                    finish2(pend2.pop(0))
                pa = g.ps[:, half * 2048:(half + 1) * 2048]
                pr = [k.R("ps", half * 4 + t) for t in range(4)]
                k.op("dve", lambda e: e.tensor_tensor(out=xt[s][:], in0=pa, in1=xt[s][:], op=ALU.add), reads=pr + [k.R("txt", s)], writes=[k.R("txt", s)])
                k.dma("sp", S["xmid"][tsl, :], xt[s][:], reads=[k.R("txt", s)], sem="txo%d" % s, store=True)
                k.op("act", lambda e: e.activation(out=junk[:], in_=xt[s][:], func=AF.Square, accum_out=stat[:, tt:tt + 1]),
                     reads=[k.R("txt", s)], writes=[k.R("tjunk"), k.R("tst", tt)])
                k.op("act", lambda e: e.activation(out=stat[:, 16 + tt:17 + tt], in_=stat[:, tt:tt + 1], func=AF.Sqrt, scale=1.0 / D, bias=g.epsb[:, 0:1]),
                     reads=[k.R("tst", tt), k.R("epsb")], writes=[k.R("tst2", tt)])
                k.op("dve", lambda e: e.reciprocal(out=stat[:, 32 + tt:33 + tt], in_=stat[:, 16 + tt:17 + tt]), reads=[k.R("tst2", tt)], writes=[k.R("tst3", tt)])
                k.op("dve", lambda e: e.scalar_tensor_tensor(out=xns[s][:], in0=xt[s][:], scalar=stat[:, 32 + tt:33 + tt], in1=gB2[:], op0=ALU.mult, op1=ALU.mult),
                     reads=[k.R("txt", s), k.R("tst3", tt), k.R("gB2")], writes=[k.R("txn", s)])
                pend2.append((tt, s, half))
            while pend2:
                finish2(pend2.pop(0))
            k.barrier()
        k.fence()
        with ExitStack() as e3:
            wG = [sbt(k, e3, "wG%d" % i, [128, 16, 512], BF16) for i in range(2)]
            wPp = [sbt(k, e3, "wPp%d" % i, [128, 2, 512], BF16) for i in range(2)]
            pT = sbt(k, e3, "pT", [128, 2, LT], BF16)
            pt = [sbt(k, e3, "pt%d" % i, [128, 256], F32) for i in range(2)]
            sg = [sbt(k, e3, "sg%d" % i, [128, 512], F32) for i in range(2)]
            xm2 = [sbt(k, e3, "xm2%d" % i, [128, 512], F32) for i in range(2)]
            for tt in range(NT):
                s = tt % 2
                tsl = slice(tt * 128, (tt + 1) * 128)
                k.dma("sp", pt[s][:], p_src[tsl, :], writes=[k.R("pt", s)], sem="pt%d" % s)
                b = k.nxt("bank", 8)
                for j in range(2):
                    k.op("pe", lambda e: e.transpose(out=psb(g, b, 128, j * 128), in_=pt[s][:, j * 128:(j + 1) * 128], identity=g.ident),
                         reads=[k.R("pt", s), k.R("cf")], writes=[k.R("ps", b)], inc=(j == 1))
                k.op("act", lambda e: e.copy(out=pT[:, :, tsl], in_=psb(g, b, 256).rearrange("p (a c) -> p a c", a=2)), reads=[k.R("ps", b)], writes=[k.R("pT")])
            for eg in range(4):
                ws = eg % 2
                csl = slice(eg * 512, (eg + 1) * 512)
                k.dma("pool", wG[ws][:], P["w_ple_gate"][li][:, csl].rearrange("(kt p) c -> p kt c", p=128), writes=[k.R("wG", ws)], sem="wG%d" % ws)
                k.dma("pool", wPp[ws][:], P["w_ple_proj"][li][:, csl].rearrange("(kt p) c -> p kt c", p=128), writes=[k.R("wPp", ws)], sem="wPp%d" % ws)
                for tt in range(NT):
                    tsl = slice(tt * 128, (tt + 1) * 128)
                    s = k.nxt("sg", 2)
                    k.dma("sp", xm2[s][:], S["xmid"][tsl, csl], writes=[k.R("xm2", s)], sem="xm2%d" % s)
                    b1 = k.nxt("bank", 8)
                    for kt in range(16):
                        k.op("pe", lambda e: e.matmul(out=psb(g, b1), lhsT=mT[:, kt, tsl], rhs=wG[ws][:, kt, :], start=(kt == 0), stop=(kt == 15)),
                             reads=[k.R("mT", tt), k.R("wG", ws)], writes=[k.R("ps", b1)], inc=(kt == 15))
                    b2 = k.nxt("bank", 8)
                    for kt in range(2):
                        k.op("pe", lambda e: e.matmul(out=psb(g, b2), lhsT=pT[:, kt, tsl], rhs=wPp[ws][:, kt, :], start=(kt == 0), stop=(kt == 1)),
                             reads=[k.R("pT"), k.R("wPp", ws)], writes=[k.R("ps", b2)], inc=(kt == 1))
                    k.op("act", lambda e: e.activation(out=sg[s][:], in_=psb(g, b1), func=AF.Sigmoid), reads=[k.R("ps", b1)], writes=[k.R("sg", s)])
                    k.op("dve", lambda e: e.tensor_tensor(out=sg[s][:], in0=sg[s][:], in1=psb(g, b2), op=ALU.mult), reads=[k.R("sg", s), k.R("ps", b2)], writes=[k.R("sg", s)])
                    k.op("dve", lambda e: e.tensor_tensor(out=sg[s][:], in0=sg[s][:], in1=xm2[s][:], op=ALU.add), reads=[k.R("sg", s), k.R("xm2", s)], writes=[k.R("sg", s)])
                    k.dma("sp", x_dst[tsl, csl], sg[s][:], reads=[k.R("sg", s)], sem="sg%d" % s, store=True)
    k.barrier()


def ysel_build(k, g, S, tiles, fence=True):
    s0 = g.poolc[:, 136:137]
    s1 = g.poolc[:, 137:138]
    if fence:
        k.fence(("sp",))
    else:
        d = k.dsems["cc_y"]
        k._wait("sp", [(d.sem, d.count - 1)])
    for q in range(2):
        for i in tiles:
            kt = q * 18 + i
            s = k.nxt("Gl", 2)
            k.dma("sp", g.Gl[s][:], S["yg%d" % (i // 3)][q * 384 + (i % 3) * 128:q * 384 + (i % 3 + 1) * 128, :], writes=[k.R("Gl", s)], sem="Gl%d" % s)
            k.op("dve", lambda e: e.tensor_scalar(out=g.Ysel[:, kt, :], in0=g.Gl[s][:, 0:LT], scalar1=s0, scalar2=None, op0=ALU.mult),
                 reads=[k.R("Gl", s), k.R("poolc")], writes=[k.R("Ysel", kt)])
            k.op("dve", lambda e: e.scalar_tensor_tensor(out=g.Ysel[:, kt, :], in0=g.Gl[s][:, LT:L], scalar=s1, in1=g.Ysel[:, kt, :], op0=ALU.mult, op1=ALU.add),
                 reads=[k.R("Gl", s), k.R("poolc"), k.R("Ysel", kt)], writes=[k.R("Ysel", kt)])


def build_program(nc, depth=DEPTH, kind="Internal", phases=("A", "P", "AT", "S", "X", "T"), ncores=8):
    x_ap = nc.dram_tensor("x", [LT, D], F32, kind="ExternalInput").ap()
    p_ap = nc.dram_tensor("p", [depth, LT, 256], F32, kind="ExternalInput").ap()
    c_ap = nc.dram_tensor("consts", [128, 640], F32, kind="ExternalInput").ap()
    sel_ap = nc.dram_tensor("sel", [128, 144], F32, kind="ExternalInput").ap()
    out_ap = nc.dram_tensor("out", [LT, D], F32, kind="ExternalOutput").ap()
    P = {n: nc.dram_tensor(n, [depth] + shp[1:], F32, kind="ExternalInput").ap() for n, shp in PARAM_SHAPES.items()}
    S = alloc_scratch(nc, kind=kind)
    if kind != "Internal":
        for n in ["hTb0", "hTb1", "hTg0", "hTg1", "ssp", "ssg"] + ["ysend%d" % c for c in range(6)] + ["yg%d" % c for c in range(6)]:
            shp, dt = list(S[n].shape), S[n].dtype
            S[n] = nc.dram_tensor(n + "_i", shp, dt, kind="Internal").ap()
    with ExitStack() as es:
        k = K(nc, es)
        g = Ctx()
        g.pairs = [[2 * i, 2 * i + 1] for i in range(ncores // 2)]
        setup_globals(k, es, g, c_ap, sel_ap)
        for li in range(depth):
            k.sfx = "_L%d" % li
            x_src = x_ap if li == 0 else S["x1"]
            x_dst = out_ap if li == depth - 1 else S["x1"]
            if "A" in phases:
                phase_A(k, g, li, x_src, P, S)
            k.fence()
            em = ExitStack()
            g.mT = sbt(k, em, "mT", [128, 16, LT], BF16)
            el = ExitStack()
            g.el = el
            g.Ysel = sbt(k, el, "Ysel", [128, 36, LT], BF16)
            g.Gl = [sbt(k, el, "Gl%d" % i, [128, L], BF16) for i in range(2)]
            phase_S(k, g, li, P, S)
            for c in range(2):
                k.collective("AllGather", S["ysend%d" % c], S["yg%d" % c], g.pairs, "y")
            k.collective("AllGather", S["ssp"], S["ssg"], g.pairs, "s")
            phase_P(k, g, li, P, S)
            for c in range(2, 5):
                k.collective("AllGather", S["ysend%d" % c], S["yg%d" % c], g.pairs, "y")
            phase_AT(k, g, li, P, S)
            k.collective("AllGather", S["ysend5"], S["yg5"], g.pairs, "y")
            ysel_build(k, g, S, range(0, 15), fence=False)
            k.fence()
            if "T" in phases:
                phase_T(k, g, li, x_src, x_dst, p_ap[li], P, S)
            em.close()
            k.fence()
        k.barrier()
    return k


def _sel_table(r):
    t = np.zeros((128, 144), np.float32)
    for gl in range(2):
        w = (2, 4, 8, 16)[2 * r + gl]
        j = int(np.log2(w)) - 1
        t[:, gl * 4 + j] = 1.0 / w
        t[:, 8 + (gl * 4 + j) * 16: 8 + (gl * 4 + j) * 16 + 16] = 1.0 / np.minimum(np.arange(16) + 1, w)
    t[:, 136 + r] = 1.0
    return t


def _core_params(inputs, r, depth=DEPTH):
    f = lambda a: np.ascontiguousarray(a[:depth], dtype=np.float32)
    w_in = inputs["w_in"]
    o = np.cumsum([0, 2048, 3072, 32, 2048, 2048, 1536, 1536, 1536, 512])
    ZS, XBC, DT_, UP, ZP, Q_, K_, V_, ZA, G_ = [int(v) for v in o]
    heads = [gi * 4 + 2 * r + jl for gi in range(3) for jl in range(2)]
    cols = []
    cols += list(range(ZS + r * 1024, ZS + (r + 1) * 1024))
    cols += list(range(XBC + r * 1024, XBC + (r + 1) * 1024))
    cols += list(range(XBC + 2048 + r * 256, XBC + 2048 + (r + 1) * 256))
    cols += list(range(XBC + 2560 + r * 256, XBC + 2560 + (r + 1) * 256))
    cols += list(range(UP + r * 1024, UP + (r + 1) * 1024))
    cols += list(range(ZP + r * 1024, ZP + (r + 1) * 1024))
    for base in (Q_, K_):
        for h in heads:
            cols += list(range(base + h * 128, base + (h + 1) * 128))
    cols += list(range(ZA + 2 * r * 128, ZA + (2 * r + 2) * 128))
    cols += list(range(DT_ + r * 16, DT_ + (r + 1) * 16))
    for gi in range(3):
        cols += list(range(V_ + (gi * 4 + 2 * r) * 128, V_ + (gi * 4 + 2 * r + 2) * 128))
    cols += list(range(G_, G_ + 6144))
    cols = np.asarray(cols)
    assert cols.size == NCOL
    xbc_idx = np.concatenate([np.arange(r * 1024, (r + 1) * 1024), 2048 + np.arange(r * 256, (r + 1) * 256), 2560 + np.arange(r * 256, (r + 1) * 256)])
    m = {}
    m["norm_g"] = f(inputs["norm_g"])
    m["w_in"] = np.ascontiguousarray(w_in[:depth][:, :, cols], dtype=np.float32)
    m["conv_w"] = np.ascontiguousarray(inputs["conv_w"][:depth][:, :, xbc_idx], dtype=np.float32)
    m["conv_b"] = np.ascontiguousarray(inputs["conv_b"][:depth][:, xbc_idx], dtype=np.float32)
    for n in ("dt_bias", "a_log", "d_skip"):
        m[n] = np.ascontiguousarray(inputs[n][:depth][:, r * 16:(r + 1) * 16], dtype=np.float32)
    m["ssd_norm_g"] = np.ascontiguousarray(inputs["ssd_norm_g"][:depth][:, r * 1024:(r + 1) * 1024], dtype=np.float32)
    rows = []
    for q in range(2):
        rows.append(inputs["w_br_ssd"][:depth][:, q * 1024:(q + 1) * 1024])
        rows.append(inputs["w_br_pool"][:depth][:, q * 1024:(q + 1) * 1024])
        rows.append(inputs["w_br_attn"][:depth][:, q * 256:(q + 1) * 256])
    m["w_br"] = np.ascontiguousarray(np.concatenate(rows, axis=1), dtype=np.float32)
    m["pool_w"] = np.ascontiguousarray(inputs["pool_w"][:depth][:, 2 * r:2 * r + 2], dtype=np.float32)
    m["pool_scale"] = np.ascontiguousarray(inputs["pool_scale"][:depth][:, r * 1024:(r + 1) * 1024], dtype=np.float32)
    m["qk_g"] = np.ascontiguousarray(np.stack([inputs["q_norm_g"][:depth], inputs["k_norm_g"][:depth]], axis=1), dtype=np.float32)
    for n in ("w_out", "ple_norm_g", "w_ple_gate", "w_ple_proj"):
        m[n] = f(inputs[n])
    m["consts"] = _consts()
    m["sel"] = _sel_table(r)
    return m


def make_inmaps(inputs, depth=DEPTH, nb=4):
    par = [_core_params(inputs, r, depth) for r in range(2)]
    maps = []
    for b in range(nb):
        for r in range(2):
            m = dict(par[r])
            m["x"] = np.ascontiguousarray(inputs["x"][b, r * LT:(r + 1) * LT], dtype=np.float32)
            m["p"] = np.ascontiguousarray(inputs["p"][:depth, b, r * LT:(r + 1) * LT], dtype=np.float32)
            maps.append(m)
    return maps


def kernel(**inputs):
    nc = bass.Bass("TRN2", target_bir_lowering=False)
    build_program(nc)
    maps = make_inmaps(inputs)
    res = run_bass_kernel_spmd(nc, maps, core_ids=list(range(8)))
    out = np.zeros((4, L, D), np.float32)
    for b in range(4):
        for r in range(2):
            out[b, r * LT:(r + 1) * LT] = np.asarray(res.results[2 * b + r]["out"], dtype=np.float32)
    return out
```
